# Optimizing a Trainium2 kernel written in Bass

```python
import jax, jax.numpy as jnp
from jax import lax
import numpy as np

D_MODEL = 4096
BATCH = 32
SEQ = 256
DEPTH = 1
DEC_BATCH = 8
DEC_SEQ = 2048
PAST_LEN = 512

GRID_W = 64
N_HEADS = 32
N_KV_HEADS = 8
HEAD_DIM = D_MODEL // N_HEADS
Q_PER_KV = N_HEADS // N_KV_HEADS
WINDOW = 128
BLOCK = 128
CONV_DIM = D_MODEL // 2
CONV_K = 3
D_FF = 256 * ((8 * D_MODEL // 3 + 255) // 256)
ROPE_THETA = 10000.0
EPS = 1e-6
N_MOD = 6
Q_W = N_HEADS * HEAD_DIM
KV_W = N_KV_HEADS * HEAD_DIM
IN_W = Q_W + 2 * KV_W + 3 * CONV_DIM + 2 * D_MODEL
NEG = -1e30

kernel_name = 'hybrid_dit_swa_shortconv_step'


def _rmsnorm(x, g):
    xf = x.astype(jnp.float32)
    y = xf * lax.rsqrt(jnp.mean(xf * xf, axis=-1, keepdims=True) + EPS)
    return (y * g.astype(jnp.float32)).astype(x.dtype)


def _modulation(cond, w_mod, b_mod):
    m = jax.nn.silu(cond) @ w_mod + b_mod
    return m.reshape(cond.shape[:-1] + (N_MOD, D_MODEL))


def _dwconv3(x, w):
    L = x.shape[1]
    xp = jnp.pad(x, ((0, 0), (1, 1), (0, 0)))
    return xp[:, :L] * w[0] + xp[:, 1:L + 1] * w[1] + xp[:, 2:] * w[2]


def _axial_rope_tables(L):
    rows = L // GRID_W
    row = jnp.repeat(jnp.arange(rows), GRID_W).astype(jnp.float32)
    col = jnp.tile(jnp.arange(GRID_W), rows).astype(jnp.float32)
    n_freq = HEAD_DIM // 4
    inv = ROPE_THETA ** (-jnp.arange(n_freq, dtype=jnp.float32) / n_freq)
    ang = jnp.concatenate([row[:, None] * inv, col[:, None] * inv], axis=-1)
    return jnp.cos(ang), jnp.sin(ang)


def _apply_axial_rope(x, cos, sin):
    B, L, H, _ = x.shape
    nf = HEAD_DIM // 4
    xf = x.astype(jnp.float32).reshape(B, L, H, 2, 2, nf)
    x1, x2 = xf[..., 0, :], xf[..., 1, :]
    c = cos.reshape(L, 1, 2, nf)
    s = sin.reshape(L, 1, 2, nf)
    out = jnp.stack([x1 * c - x2 * s, x1 * s + x2 * c], axis=-2)
    return out.reshape(B, L, H, HEAD_DIM).astype(x.dtype)


def _sink_attend(q_blk, k, v, sink, mask=None):
    B, Q = q_blk.shape[:2]
    qg = q_blk.reshape(B, Q, N_KV_HEADS, Q_PER_KV, HEAD_DIM)
    s = jnp.einsum('bqkgd,btkd->bkgqt', qg, k, preferred_element_type=jnp.float32) * (HEAD_DIM ** -0.5)
    if mask is not None:
        s = jnp.where(mask, s, NEG)
    sk = sink.astype(jnp.float32).reshape(1, N_KV_HEADS, Q_PER_KV, 1, 1)
    m = jnp.maximum(jnp.max(s, axis=-1, keepdims=True), sk)
    p = jnp.exp(s - m)
    p = (p / (jnp.sum(p, axis=-1, keepdims=True) + jnp.exp(sk - m))).astype(v.dtype)
    o = jnp.einsum('bkgqt,btkd->bqkgd', p, v)
    return o.reshape(B, Q, Q_W)


def _context_attention(q, k, v, sink):
    B, S = q.shape[:2]
    nb = S // BLOCK
    qb = q.reshape(B, nb, BLOCK, N_HEADS, HEAD_DIM).transpose(1, 0, 2, 3, 4)
    out = lax.map(lambda qi: _sink_attend(qi, k, v, sink), qb)
    return out.transpose(1, 0, 2, 3).reshape(B, S, Q_W)


def _latent_attention(q, k, v, k_ctx, v_ctx, sink):
    B, L = q.shape[:2]
    P = k_ctx.shape[1]
    nb = L // BLOCK
    pad = ((0, 0), (BLOCK, BLOCK), (0, 0), (0, 0))
    kp = jnp.pad(k, pad)
    vp = jnp.pad(v, pad)
    ctx_mask = jnp.ones((BLOCK, P), dtype=bool)

    def block(b):
        start = b * BLOCK
        qi = lax.dynamic_slice_in_dim(q, start, BLOCK, axis=1)
        kb = lax.dynamic_slice_in_dim(kp, start, 3 * BLOCK, axis=1)
        vb = lax.dynamic_slice_in_dim(vp, start, 3 * BLOCK, axis=1)
        qpos = start + jnp.arange(BLOCK)
        kpos = start - BLOCK + jnp.arange(3 * BLOCK)
        band = ((kpos[None, :] >= 0) & (kpos[None, :] < L)
                & (jnp.abs(qpos[:, None] - kpos[None, :]) <= WINDOW))
        keys = jnp.concatenate([kb, k_ctx], axis=1)
        vals = jnp.concatenate([vb, v_ctx], axis=1)
        mask = jnp.concatenate([band, ctx_mask], axis=1)
        return _sink_attend(qi, keys, vals, sink, mask)

    out = lax.map(block, jnp.arange(nb))
    return out.transpose(1, 0, 2, 3).reshape(B, L, Q_W)


def _layer(x, mod, attend, w_in, w_sconv, w_attn_o, w_conv_o, w_mix_out,
           g_pre_mix, g_post_mix, g_pre_ffn, w_ffn_up, w_ffn_conv, w_ffn_down, g_post_ffn):
    B, L = x.shape[:2]
    shift_a, scale_a, gate_a, shift_f, scale_f, gate_f = [mod[:, i][:, None, :] for i in range(N_MOD)]

    h = _rmsnorm(x, g_pre_mix) * (1 + scale_a) + shift_a
    proj = h @ w_in
    i0 = Q_W
    i1 = i0 + KV_W
    i2 = i1 + KV_W
    i3 = i2 + CONV_DIM
    i4 = i3 + CONV_DIM
    i5 = i4 + CONV_DIM
    i6 = i5 + D_MODEL
    q = proj[..., :i0].reshape(B, L, N_HEADS, HEAD_DIM)
    k = proj[..., i0:i1].reshape(B, L, N_KV_HEADS, HEAD_DIM)
    v = proj[..., i1:i2].reshape(B, L, N_KV_HEADS, HEAD_DIM)
    cb, cc, ch = proj[..., i2:i3], proj[..., i3:i4], proj[..., i4:i5]
    g_att, g_conv = proj[..., i5:i6], proj[..., i6:]

    a = attend(q, k, v)
    sc = cb * _dwconv3(cc * ch, w_sconv)
    merged = jax.nn.sigmoid(g_att) * (a @ w_attn_o) + jax.nn.sigmoid(g_conv) * (sc @ w_conv_o)
    x = x + gate_a * _rmsnorm(merged @ w_mix_out, g_post_mix)

    h = _rmsnorm(x, g_pre_ffn) * (1 + scale_f) + shift_f
    u = h @ w_ffn_up
    gt, val = u[..., :D_FF], u[..., D_FF:]
    f = (jax.nn.silu(_dwconv3(gt, w_ffn_conv)) * val) @ w_ffn_down
    x = x + gate_f * _rmsnorm(f, g_post_ffn)
    return x, k, v


def setup_inputs(seed: int = 0) -> dict:
    key = jax.random.key(seed)
    ks = jax.random.split(key, 24)
    f32 = jnp.float32

    def nrm(k, shape, s):
        return jax.random.normal(k, shape, f32) * s

    def gain(k):
        return 1.0 + 0.05 * jax.random.normal(k, (DEPTH, D_MODEL), f32)

    return {
        'x_prompt': nrm(ks[0], (BATCH, SEQ, D_MODEL), 1.0),
        'x_sample': nrm(ks[1], (DEC_BATCH, DEC_SEQ, D_MODEL), 1.0),
        'cache_k': nrm(ks[2], (DEC_BATCH, DEPTH, PAST_LEN, N_KV_HEADS, HEAD_DIM), 1.0),
        'cache_v': nrm(ks[3], (DEC_BATCH, DEPTH, PAST_LEN, N_KV_HEADS, HEAD_DIM), 1.0),
        'c': nrm(ks[4], (DEC_BATCH, D_MODEL), 1.0),
        'c_ctx': nrm(ks[5], (D_MODEL,), 1.0),
        'w_mod': nrm(ks[6], (DEPTH, D_MODEL, N_MOD * D_MODEL), 0.5 * D_MODEL ** -0.5),
        'b_mod': nrm(ks[7], (DEPTH, N_MOD * D_MODEL), 0.01),
        'g_pre_mix': gain(ks[8]),
        'w_in': nrm(ks[9], (DEPTH, D_MODEL, IN_W), D_MODEL ** -0.5),
        'w_sconv': nrm(ks[10], (DEPTH, CONV_K, CONV_DIM), CONV_K ** -0.5),
        'attn_sink': nrm(ks[11], (DEPTH, N_HEADS), 1.0),
        'w_attn_o': nrm(ks[12], (DEPTH, Q_W, D_MODEL), Q_W ** -0.5),
        'w_conv_o': nrm(ks[13], (DEPTH, CONV_DIM, D_MODEL), CONV_DIM ** -0.5),
        'w_mix_out': nrm(ks[14], (DEPTH, D_MODEL, D_MODEL), D_MODEL ** -0.5),
        'g_post_mix': gain(ks[15]),
        'g_pre_ffn': gain(ks[16]),
        'w_ffn_up': nrm(ks[17], (DEPTH, D_MODEL, 2 * D_FF), D_MODEL ** -0.5),
        'w_ffn_conv': nrm(ks[18], (DEPTH, CONV_K, D_FF), CONV_K ** -0.5),
        'w_ffn_down': nrm(ks[19], (DEPTH, D_FF, D_MODEL), D_FF ** -0.5),
        'g_post_ffn': gain(ks[20]),
    }


def reference(x_prompt, x_sample, cache_k, cache_v, c, c_ctx, w_mod, b_mod, g_pre_mix, w_in,
              w_sconv, attn_sink, w_attn_o, w_conv_o, w_mix_out, g_post_mix, g_pre_ffn,
              w_ffn_up, w_ffn_conv, w_ffn_down, g_post_ffn):
    cos, sin = _axial_rope_tables(x_sample.shape[1])
    xp, xs = x_prompt, x_sample
    new_k, new_v = [], []
    for l in range(DEPTH):
        lw = (w_in[l], w_sconv[l], w_attn_o[l], w_conv_o[l], w_mix_out[l], g_pre_mix[l],
              g_post_mix[l], g_pre_ffn[l], w_ffn_up[l], w_ffn_conv[l], w_ffn_down[l], g_post_ffn[l])
        sink = attn_sink[l]
        mod_ctx = _modulation(c_ctx[None, :], w_mod[l], b_mod[l])
        mod_lat = _modulation(c, w_mod[l], b_mod[l])

        def ctx_attend(q, k, v, sink=sink):
            return _context_attention(q, k, v, sink)

        def lat_attend(q, k, v, sink=sink, k_ctx=cache_k[:, l], v_ctx=cache_v[:, l]):
            return _latent_attention(_apply_axial_rope(q, cos, sin), _apply_axial_rope(k, cos, sin),
                                     v, k_ctx, v_ctx, sink)

        xp, k_l, v_l = _layer(xp, mod_ctx, ctx_attend, *lw)
        xs, _, _ = _layer(xs, mod_lat, lat_attend, *lw)
        new_k.append(k_l)
        new_v.append(v_l)
    k_state = jnp.stack(new_k, axis=1)
    v_state = jnp.stack(new_v, axis=1)
    return (xp, xs, k_state, v_state)
```

```python
import os
from contextlib import ExitStack
import numpy as np
import concourse.bass as bass
import concourse.mybir as mybir
from concourse.bass_utils import run_bass_kernel_spmd

F32 = mybir.dt.float32
BF16 = mybir.dt.bfloat16
AF = mybir.ActivationFunctionType
ALU = mybir.AluOpType
AX = mybir.AxisListType

D = 4096
KC = 32
DFF = 11008
FC = 86
NH = 32
NKV = 8
PAST = 512
EPS = 1e-6
SCALE = 128.0 ** -0.5
I_K, I_V, I_CB, I_CC, I_CH, I_GA, I_GC = 4096, 5120, 6144, 8192, 10240, 12288, 16384
NEGBIG = -1e30
PRECONV_EVERY = int(os.environ.get('PRECONV_EVERY', '4'))


class Dep:
    __slots__ = ("w", "r", "excl")

    def __init__(self, excl=False):
        self.w = None
        self.r = {}
        self.excl = excl


def alias(new_deps, old_deps):
    acc = {}
    for d in old_deps:
        evs = list(d.r.values())
        if d.w is not None:
            evs.append(d.w)
        for e in evs:
            k = id(e[0])
            if k not in acc or acc[k][1] < e[1]:
                acc[k] = e
    for d in new_deps:
        for k, e in acc.items():
            if k not in d.r or d.r[k][1] < e[1]:
                d.r[k] = e


class Planner:
    def __init__(self, nc, stack):
        self.nc = nc
        self.stack = stack
        self.streams = {"pe": [], "act": [], "dve": [], "pool": [], "sp": []}
        self.esem = {}
        self.nsem = 0
        self.owner = {}

    def new_sem(self, name, owner=None):
        s = self.stack.enter_context(self.nc.semaphore(f"{name}{self.nsem}"))
        self.nsem += 1
        self.owner[id(s)] = owner
        return s

    def dma_sem(self, name):
        return [self.new_sem(name), 0]

    def _eng_event(self, eng):
        st = self.esem.get(eng)
        if st is None or st[1] >= 12000:
            st = [self.new_sem(eng, owner=eng), 0]
            self.esem[eng] = st
        st[1] += 1
        return (st[0], st[1])

    def op(self, eng, fn, reads=(), writes=(), dsem=None, extra=()):
        waits = {}

        def add(ev):
            if ev is None:
                return
            k = id(ev[0])
            if k not in waits or waits[k][1] < ev[1]:
                waits[k] = ev

        ex = [d for d in reads if d.excl]
        if ex:
            reads = [d for d in reads if not d.excl]
            writes = list(writes) + ex
        for d in reads:
            add(d.w)
        for d in writes:
            add(d.w)
            for e in d.r.values():
                add(e)
        for e in extra:
            add(e)
        if dsem is not None:
            dsem[1] += 16
            ev = (dsem[0], dsem[1])
            amt = 16
        else:
            ev = self._eng_event(eng)
            amt = 1
        k = id(ev[0])
        for d in reads:
            if k not in d.r or d.r[k][1] < ev[1]:
                d.r[k] = ev
        for d in writes:
            d.w = ev
            d.r = {}
        self.streams[eng].append((fn, list(waits.values()), ev, amt))
        return ev

    def emit(self, name, eng):
        waited = {}
        for fn, waits, ev, amt in self.streams[name]:
            for (s, v) in waits:
                k = id(s)
                if waited.get(k, 0) >= v:
                    continue
                if name == "pe" and self.owner.get(k) == "pe":
                    continue
                eng.wait_ge(s, v)
                waited[k] = v
            ins = fn(eng)
            ins.then_inc(ev[0], amt)


def build_program(n_ptiles=2, n_stiles=4, do_b=True):
    nc = bass.Bass("TRN2", target_bir_lowering=False)
    stack = ExitStack()
    P = Planner(nc, stack)

    def din(name, shape):
        return nc.dram_tensor(name, shape, F32, kind="ExternalInput").ap()

    def dout(name, shape):
        return nc.dram_tensor(name, shape, F32, kind="ExternalOutput").ap()

    xp = din("xp", [1024, D])
    xs = din("xs", [2048, D])
    ck = din("ck", [PAST, 1024])
    cv = din("cv", [PAST, 1024])
    condT_d = din("condT", [128, 64])
    w_mod = din("w_mod", [D, 6 * D])
    b_modT_d = din("b_modT", [128, 192])
    gvec_d = din("gvec", [128, 128])
    w_in = din("w_in", [D, 20480])
    w_sconvT_d = din("w_sconvT", [128, 48])
    sinkb_d = din("sinkb", [128, 32])
    w_attn_o = din("w_attn_o", [D, D])
    w_conv_o = din("w_conv_o", [2048, D])
    w_mix_out = din("w_mix_out", [D, D])
    w_ffn_up = din("w_ffn_up", [D, 2 * DFF])
    w_ffn_convT_d = din("w_ffn_convT", [128, 3 * FC])
    w_ffn_down = din("w_ffn_down", [DFF, D])
    ident_d = din("ident", [128, 128])
    perm_d = din("perm", [128, 128])
    mask_d = din("maskc", [128, 384])
    ropeC_d = din("ropeC", [128, 2048])
    ropeS_d = din("ropeS", [128, 2048])

    yp = dout("yp", [1024, D])
    ys = dout("ys", [2048, D])
    nk_o = dout("nk", [1024, 1024])
    nv_o = dout("nv", [1024, 1024])

    NSCR = 288
    wscr_t = [nc.dram_tensor(f"wscr{i}", [96, 128, 8192], BF16, kind="Internal").ap() for i in range(3)]
    x1p = nc.dram_tensor("x1p", [1024, D], F32, kind="Internal").ap()
    x1s = nc.dram_tensor("x1s", [2048, D], F32, kind="Internal").ap()
    grow = nc.dram_tensor("grow", [4, D], F32, kind="Internal").ap()
    fscr = nc.dram_tensor("fscr", [512, D], F32, kind="Internal").ap()

    def wview(w):
        return w.rearrange("(kc p) n -> p kc n", p=128)

    wv_mod, wv_in, wv_ao, wv_co, wv_mix, wv_up = (
        wview(w_mod), wview(w_in), wview(w_attn_o), wview(w_conv_o), wview(w_mix_out), wview(w_ffn_up))
    wv_down = wview(w_ffn_down)

    def sb(name, shape, dt):
        return stack.enter_context(nc.sbuf_tensor(name, shape, dt))

    R = sb("R", [128, 69632], BF16)
    RW = [sb(f"RW{i}", [128, 32, 256], BF16) for i in range(2)]
    RK = sb("RK", [128, 4096], F32)
    ident_f = sb("ident_f", [128, 128], F32)
    ident_b = sb("ident_b", [128, 128], BF16)
    perm_f = sb("perm_f", [128, 128], F32)
    perm_b = sb("perm_b", [128, 128], BF16)
    mask_sb = sb("mask_sb", [128, 384], F32)
    ropeC = sb("ropeC_sb", [128, 768], F32)
    ropeS = sb("ropeS_sb", [128, 768], F32)
    condT = sb("condT_sb", [128, 64], F32)
    scT = sb("scT", [128, 64], BF16)
    b_modT = sb("b_modT_sb", [128, 192], F32)
    gvec = sb("gvec_sb", [128, 128], F32)
    modc = [sb(f"modc{c}", [128, 192], F32) for c in range(2)]
    A1 = [sb(f"A1_{c}", [128, 32], F32) for c in range(2)]
    A2 = [sb(f"A2_{c}", [128, 32], F32) for c in range(2)]
    Gf = sb("Gf", [128, 4, 32], F32)
    GT = sb("GT", [32, 4, 128], F32)
    w_sconvT = sb("w_sconvT_sb", [128, 48], F32)
    w_ffn_convT = sb("w_ffn_convT_sb", [128, 3 * FC], F32)
    sinkb = sb("sinkb_sb", [128, 32], F32)
    negsink = sb("negsink", [128, 32], F32)
    stat = sb("stat", [128, 64], F32)
    stat2 = sb("stat2", [128, 4, 16], F32)
    astat = sb("astat", [128, 64], F32)
    halo4 = sb("halo4", [128, 2, 4], F32)
    kvst = sb("kvst", [128, 4, 256], F32)
    junk = sb("junk", [128, 256], F32)
    mask_b = sb("mask_b", [128, 384], BF16)
    ones_f = sb("ones_f", [1, 128], F32)
    hTh = sb("hTh", [128, 32, 2], BF16)

    pdb = [stack.enter_context(nc.psum_tensor(f"pd{i}", [128, 1024], F32)) for i in range(4)]
    ps = [pdb[i // 2][:, (i % 2) * 512:(i % 2 + 1) * 512] for i in range(8)]
    psd = [Dep(excl=True) for _ in range(8)]
    rr = [0]

    def nextps():
        i = rr[0] % 8
        rr[0] += 1
        return ps[i], psd[i]

    def psbf(p):
        return p[:, :].bitcast(BF16)

    def rview(off, n):
        return R[:, off:off + n]

    hT = rview(0, 24576).rearrange("p (c t) -> p c t", c=32)
    scv = rview(24576, 8192).rearrange("p (c t) -> p c t", c=16)
    mo32 = rview(0, 32768).bitcast(F32).rearrange("p (b n) -> p b n", b=4)
    qa = rview(32768, 16384).rearrange("p (c t) -> p c t", c=32)
    xblk0 = rview(32768, 8192).bitcast(F32)
    xn_v = rview(40960, 4096)
    GrowA = rview(40960, 8192).bitcast(F32)
    OA = 49152
    kT = rview(OA, 6144).rearrange("p (c t) -> p c t", c=8)
    vtok = rview(OA + 6144, 6144).rearrange("p (b n) -> p b n", b=6)
    kcT = rview(OA + 12288, 4096).rearrange("p (c t) -> p c t", c=8)
    vc = rview(OA + 16384, 4096).rearrange("p (b n) -> p b n", b=4)
    merged = rview(OA, 16384).rearrange("p (c t) -> p c t", c=32)
    xblk1 = rview(OA, 8192).bitcast(F32)
    actv = rview(0, 44032).rearrange("p (c t) -> p c t", c=FC)
    OH2 = 44032
    hT2 = rview(OH2, 16448).rearrange("p (c t) -> p c t", c=32)
    f_sb = rview(OH2, 16384).rearrange("p (b n) -> p b n", b=4)
    bx0 = rview(0, 8192).bitcast(F32)
    bxn = rview(8192, 4096)
    bx1 = rview(8192, 8192).bitcast(F32)
    btmp = rview(16384, 8192).bitcast(F32)
    GrowB = rview(24576, 8192).bitcast(F32)

    wsem = [P.dma_sem("w"), P.dma_sem("w")]
    wdep = [Dep(), Dep()]
    wn = [0]


    wscr_idx = {}
    wscr_dep = {}
    pend_store = [None]
    defer = {"on": False, "cnt": 0}
    ssem = [P.dma_sem("ws"), P.dma_sem("ws")]
    pcsem = P.dma_sem("pc")
    preconv = {"list": [], "pos": 0, "active": False, "cnt": 0, "deps": []}
    for jj_ in range(FC // 2):
        preconv["list"].append((("gt", jj_), wv_up[:, :, jj_ * 256:(jj_ + 1) * 256]))
        preconv["list"].append((("val", jj_), wv_up[:, :, DFF + jj_ * 256:DFF + (jj_ + 1) * 256]))
    for nt_ in range(16):
        for kp_, (k0_, k1_) in enumerate([(0, 32), (32, 64), (64, FC)]):
            preconv["list"].append((("dn", nt_, kp_), wv_down[:, k0_:k1_, nt_ * 256:(nt_ + 1) * 256]))

    def preconv_step():
        if not preconv["active"] or preconv["pos"] >= len(preconv["list"]):
            return
        preconv["cnt"] += 1
        if preconv["cnt"] % PRECONV_EVERY:
            return
        key, src = preconv["list"][preconv["pos"]]
        preconv["pos"] += 1
        nk, ncol = src.shape[1], src.shape[2]
        idx = len(wscr_idx)
        assert idx < NSCR
        wscr_idx[key] = idx
        sd = Dep()
        wscr_dep[key] = sd
        preconv["deps"].append(sd)
        scr = wscr_t[idx // 96][idx % 96, :, 0:nk * ncol].rearrange("p (k n) -> p k n", k=nk)
        P.op("pool", lambda g, scr=scr, src=src: g.dma_start(out=scr, in_=src), dsem=pcsem)

    def wload(src, key=None):
        i = wn[0] % 2
        wn[0] += 1
        nk, ncol = src.shape[1], src.shape[2]
        dst = RW[i][:, 0:nk, 0:ncol]
        if key is not None and key in wscr_idx:
            idx = wscr_idx[key]
            scr = wscr_t[idx // 96][idx % 96, :, 0:nk * ncol].rearrange("p (k n) -> p k n", k=nk)
            P.op("pool", lambda g, dst=dst, scr=scr: g.dma_start(out=dst, in_=scr),
                 reads=[wscr_dep[key]], writes=[wdep[i]], dsem=wsem[i])
            if pend_store[0] is not None:
                pend_store[0]()
                pend_store[0] = None
            preconv_step()
            return RW[i], wdep[i]
        P.op("pool", lambda g, dst=dst, src=src: g.dma_start(out=dst, in_=src),
             writes=[wdep[i]], dsem=wsem[i])
        if pend_store[0] is not None:
            pend_store[0]()
            pend_store[0] = None
        if key is not None:
            if defer["on"]:
                defer["cnt"] += 1
                if defer["cnt"] % 16 >= 9:
                    return RW[i], wdep[i]
            idx = len(wscr_idx)
            assert idx < NSCR
            wscr_idx[key] = idx
            sd = Dep()
            wscr_dep[key] = sd
            scr = wscr_t[idx // 96][idx % 96, :, 0:nk * ncol].rearrange("p (k n) -> p k n", k=nk)

            def do_store(i=i, dst=dst, scr=scr, sd=sd):
                P.op("pool", lambda g: g.dma_start(out=scr, in_=dst), reads=[wdep[i]], writes=[sd], dsem=ssem[i])
            pend_store[0] = do_store
        return RW[i], wdep[i]

    csem = P.dma_sem("c")

    def cload(dst, src, dep):
        P.op("sp", lambda e: e.dma_start(out=dst, in_=src), writes=[dep], dsem=csem)

    def act_copy(out, in_, reads, writes):
        return P.op("act", lambda e: e.copy(out, in_), reads=reads, writes=writes)

    def dve_copy(out, in_, reads, writes):
        return P.op("dve", lambda e: e.tensor_copy(out, in_), reads=reads, writes=writes)

    evac_rr = [0]

    def any_copy(out, in_, reads, writes):
        evac_rr[0] += 1
        if evac_rr[0] % 2:
            return act_copy(out, in_, reads, writes)
        return dve_copy(out, in_, reads, writes)

    def pe_group(out, pairs, reads, writes):
        def fn(pe, out=out, pairs=pairs):
            n = len(pairs)
            for i, (l, r) in enumerate(pairs):
                ins = pe.matmul(out, l, r, start=(i == 0), stop=(i == n - 1))
            return ins
        return P.op("pe", fn, reads=reads, writes=writes)

    cd = {n: Dep() for n in ["ident_f", "perm_f", "mask", "condT", "b_modT", "gvec", "w_sconvT",
                             "w_ffn_convT", "sinkb", "ident_b", "perm_b", "scT", "negsink"]}
    cload(ident_f[:, :], ident_d, cd["ident_f"])
    cload(perm_f[:, :], perm_d, cd["perm_f"])
    cload(mask_sb[:, :], mask_d, cd["mask"])
    cload(condT[:, :], condT_d, cd["condT"])
    cload(b_modT[:, :], b_modT_d, cd["b_modT"])
    cload(gvec[:, :], gvec_d, cd["gvec"])
    cload(w_sconvT[:, :], w_sconvT_d, cd["w_sconvT"])
    cload(w_ffn_convT[:, :], w_ffn_convT_d, cd["w_ffn_convT"])
    cload(sinkb[:, :], sinkb_d, cd["sinkb"])
    final_c = (csem[0], csem[1])
    for n in ["ident_f", "perm_f", "mask", "condT", "b_modT", "gvec", "w_sconvT", "w_ffn_convT", "sinkb"]:
        cd[n].w = final_c

    dve_copy(ident_b[:, :], ident_f[:, :], [cd["ident_f"]], [cd["ident_b"]])
    cd["mask_b"] = Dep()
    cd["ones_f"] = Dep()
    dve_copy(mask_b[:, :], mask_sb[:, :], [cd["mask"]], [cd["mask_b"]])
    P.op("dve", lambda e: e.memset(ones_f[:, :], 1.0), writes=[cd["ones_f"]])
    dve_copy(perm_b[:, :], perm_f[:, :], [cd["perm_f"]], [cd["perm_b"]])
    P.op("dve", lambda e: e.tensor_scalar(negsink[:, :], sinkb[:, :], -1.0, None, ALU.mult),
         reads=[cd["sinkb"]], writes=[cd["negsink"]])
    P.op("act", lambda e: e.activation(scT[:, :], condT[:, :], AF.Silu),
         reads=[cd["condT"]], writes=[cd["scT"]])

    scT3 = scT[:, :].rearrange("p (k c) -> p k c", c=2)
    mps, mpd = nextps()
    mps3 = mps[:, 0:384].rearrange("p (f c) -> p f c", c=2)
    mod_last = None
    for t in range(96):
        wt, wd = wload(wv_mod[:, :, t * 256:(t + 1) * 256])
        for hh in range(2):
            fc = 2 * t + hh
            pairs = [(wt[:, kc, hh * 128:(hh + 1) * 128], scT3[:, kc, :]) for kc in range(KC)]
            pe_group(mps3[:, fc, :], pairs, reads=[wd, cd["scT"]], writes=[mpd])
    modd = [Dep(), Dep()]
    for c in range(2):
        P.op("dve", lambda e, c=c: e.tensor_tensor(modc[c][:, :], mps3[:, :, c], b_modT[:, :], ALU.add),
             reads=[mpd, cd["b_modT"]], writes=[modd[c]])
    gv = gvec[:, :].rearrange("p (g k) -> p g k", g=4)
    moddep = Dep()
    for c in range(2):
        P.op("dve", lambda e, c=c: e.scalar_tensor_tensor(A1[c][:, :], modc[c][:, 32:64], 1.0, gv[:, 0, :],
                                                           ALU.add, ALU.mult),
             reads=[modd[c], cd["gvec"]], writes=[moddep])
        P.op("dve", lambda e, c=c: e.scalar_tensor_tensor(A2[c][:, :], modc[c][:, 128:160], 1.0, gv[:, 2, :],
                                                           ALU.add, ALU.mult),
             reads=[modd[c], cd["gvec"]], writes=[moddep])
        P.op("dve", lambda e, c=c: e.tensor_tensor(Gf[:, c, :], modc[c][:, 64:96], gv[:, 1, :], ALU.mult),
             reads=[modd[c], cd["gvec"]], writes=[moddep])
        P.op("dve", lambda e, c=c: e.tensor_tensor(Gf[:, 2 + c, :], modc[c][:, 160:192], gv[:, 3, :], ALU.mult),
             reads=[modd[c], cd["gvec"]], writes=[moddep])
    B1 = [modc[c][:, 0:32] for c in range(2)]
    B2 = [modc[c][:, 96:128] for c in range(2)]
    growd = Dep()
    gtd = Dep()
    gsem = P.dma_sem("g")
    for r in range(4):
        tp, tpd = nextps()
        P.op("pe", lambda pe, r=r, tp=tp: pe.transpose(tp[0:32, 0:128], Gf[:, r, :], ident_f[:, :]),
             reads=[moddep, cd["ident_f"]], writes=[tpd])
        dve_copy(GT[:, r, :], tp[0:32, 0:128], [tpd], [gtd])
        P.op("sp", lambda e, r=r: e.dma_start(out=grow[r, :].rearrange("(k p) -> k p", p=128), in_=GT[:, r, :]),
             reads=[gtd], writes=[growd], dsem=gsem)

    xsem = [P.dma_sem("x"), P.dma_sem("x")]
    hsem = P.dma_sem("h")
    hsemq = [hsem, P.dma_sem("h")]
    osem = [P.dma_sem("o"), P.dma_sem("o")]
    kvsem = [P.dma_sem("kv") for _ in range(4)]
    kvd = [Dep() for _ in range(4)]
    kvn = [0]
    x1p_d = [Dep() for _ in range(8)]
    x1s_d = [Dep() for _ in range(16)]
    out_events = []

    hsd = [Dep(), Dep()]

    def h_steps(xsrc, blocks, xb, xbd, xnv, xnd, Am, Bm, hdst, hd, extra_reads, split=False):
        if not isinstance(xb, list):
            xb, xbd, xnv, xnd = [xb], [xbd], [xnv], [xnd]
        nbuf = len(xb)
        triples = []
        for bi, blk in enumerate(blocks):
            q = bi % nbuf

            def fL(blk=blk, q=q):
                for (r0, cnt, p0) in blk["rows"]:
                    P.op("sp", lambda e, r0=r0, cnt=cnt, p0=p0: e.dma_start(out=xb[q][p0:p0 + cnt, :], in_=xsrc[r0:r0 + cnt, :]),
                         reads=blk.get("srcdeps", []), writes=[xbd[q]], dsem=hsemq[q])

            def fN(blk=blk, q=q):
                n = blk["n"]
                sd = hsd[q]
                s0 = 4 * q
                P.op("act", lambda e: e.activation(xnv[q][0:n, :], xb[q][0:n, :], AF.Square, accum_out=stat[0:n, s0:s0 + 1]),
                     reads=[xbd[q]], writes=[xnd[q], sd])
                P.op("act", lambda e: e.activation(stat[0:n, s0 + 1:s0 + 2], stat[0:n, s0:s0 + 1], AF.Sqrt, bias=EPS, scale=1.0 / D),
                     reads=[sd], writes=[sd])
                P.op("dve", lambda e: e.reciprocal(stat[0:n, s0 + 2:s0 + 3], stat[0:n, s0 + 1:s0 + 2]), reads=[sd], writes=[sd])
                P.op("dve", lambda e: e.tensor_scalar(xnv[q][0:n, :], xb[q][0:n, :], stat[0:n, s0 + 2:s0 + 3], None, ALU.mult),
                     reads=[xbd[q], sd], writes=[xnd[q]])

            def fT(blk=blk, q=q):
                n = blk["n"]
                for cg in range(8):
                    tp, tpd = nextps()
                    tpb = psbf(tp)[:, 0:512].rearrange("p (i t) -> p i t", i=4)

                    def tfn(pe, cg=cg, tpb=tpb, n=n):
                        for i in range(4):
                            c = cg * 4 + i
                            ins = pe.transpose(tpb[:, i, 0:n], xnv[q][0:n, c * 128:(c + 1) * 128], ident_b[0:n, 0:n])
                        return ins
                    P.op("pe", tfn, reads=[xnd[q], cd["ident_b"]], writes=[tpd])
                    for i in range(4):
                        c = cg * 4 + i
                        for (pidx, cnt, col) in blk["dst"]:
                            o = hdst[:, c, col:col + cnt]
                            src = tpb[:, i, pidx:pidx + cnt]
                            if c % 2 == 0:
                                P.op("act", lambda e, o=o, src=src, c=c: e.activation(o, src, AF.Identity,
                                                                                        bias=Bm[:, c:c + 1], scale=Am[:, c:c + 1]),
                                     reads=[tpd, moddep] + extra_reads, writes=[hd[c]])
                            else:
                                P.op("dve", lambda e, o=o, src=src, c=c: e.tensor_scalar(o, src, Am[:, c:c + 1], Bm[:, c:c + 1],
                                                                                          ALU.mult, ALU.add),
                                     reads=[tpd, moddep] + extra_reads, writes=[hd[c]])
            triples.append((fL, fN, fT))
        if split:
            return triples
        steps = []
        for (fL, fN, fT) in triples:
            steps += [(lambda fL=fL, fN=fN: (fL(), fN())), fT]
        return steps

    reg_users = {"H": [], "S": [], "Q": [], "O": []}

    def claim(regs, deps):
        for r_ in regs:
            alias(deps, reg_users[r_])
        for r_ in regs:
            reg_users[r_] = reg_users[r_] + list(deps)

    fscr_v = fscr.rearrange("(b p) n -> p b n", p=128)
    fscrd = Dep()
    s2dB = [Dep() for _ in range(4)]
    esd = Dep()
    jdB = Dep()
    ropd_g = Dep()
    fssem = P.dma_sem("fs")
    flsem = P.dma_sem("fl")

    def epi_steps(EFb, EXb, EGb, EFd_, EXd_, EGd_, r, c_lo, xsrc, src_deps, dst, dst_deps, is_out):
        gb0 = c_lo // 128
        P.op("sp", lambda e: e.dma_start(out=EGb, in_=grow[r:r + 1, :].partition_broadcast(128)),
             reads=[growd], writes=[EGd_], dsem=gsem)

        def elf(b):
            P.op("sp", lambda e: e.dma_start(out=EFb, in_=fscr[b * 128:(b + 1) * 128, :]),
                 reads=[fscrd], writes=[EFd_], dsem=flsem)

        def elx(b):
            k = b % 2
            P.op("sp", lambda e: e.dma_start(out=EXb[k], in_=xsrc[c_lo + b * 128:c_lo + (b + 1) * 128, :]),
                 reads=[src_deps[gb0 + b]] if src_deps is not None else [], writes=[EXd_[k]], dsem=xsem[k])

        def ec(b):
            k = b % 2
            c0 = 8 + b * 4
            P.op("dve", lambda e: e.reduce_sum(stat[:, c0:c0 + 1], stat2[:, b, :], axis=AX.X),
                 reads=[s2dB[b]], writes=[esd])
            P.op("act", lambda e: e.activation(stat[:, c0 + 1:c0 + 2], stat[:, c0:c0 + 1], AF.Sqrt, bias=EPS, scale=1.0 / D),
                 reads=[esd], writes=[esd])
            P.op("dve", lambda e: e.reciprocal(stat[:, c0 + 2:c0 + 3], stat[:, c0 + 1:c0 + 2]), reads=[esd], writes=[esd])
            P.op("dve", lambda e: e.tensor_tensor(EFb, EFb, EGb, ALU.mult), reads=[EGd_], writes=[EFd_])
            P.op("dve", lambda e: e.scalar_tensor_tensor(EXb[k], EFb, stat[:, c0 + 2:c0 + 3], EXb[k], ALU.mult, ALU.add),
                 reads=[EFd_, esd], writes=[EXd_[k]])
            ev = P.op("sp", lambda e: e.dma_start(out=dst[c_lo + b * 128:c_lo + (b + 1) * 128, :], in_=EXb[k]),
                      reads=[EXd_[k]], writes=[dst_deps[gb0 + b]] if not is_out else [], dsem=osem[k])
            if is_out:
                out_events.append(ev)
        elf(0)
        elx(0)
        elx(1)
        return [[lambda: ec(0), lambda: elf(1)],
                [lambda: ec(1), lambda: elf(2), lambda: elx(2)],
                [lambda: ec(2), lambda: elf(3), lambda: elx(3)],
                [lambda: ec(3)]]

    rk_prev = [[]]

    def new_work(n):
        ds = [Dep() for _ in range(n)]
        alias(ds, rk_prev[0])
        rk_prev[0] = ds
        return ds

    hxb = [rview(24576, 8192).bitcast(F32), rview(32768, 8192).bitcast(F32)]
    hxn = [rview(40960, 4096), rview(45056, 4096)]
    FSTAf = rview(65536, 4096).bitcast(F32)
    fstA = [FSTAf[:, k * 1024:(k + 1) * 1024].rearrange("p (b n) -> p b n", b=4) for k in range(2)]
    EFa = rview(32768, 8192).bitcast(F32)
    EXa = [rview(40960, 8192).bitcast(F32), rview(49152, 8192).bitcast(F32)]
    EGa = rview(57344, 8192).bitcast(F32)

    def a_geom(kind, ti):
        c_lo = 512 * ti
        if kind == "p":
            win_lo, win_hi = c_lo, c_lo + 512
        else:
            win_lo, win_hi = max(0, c_lo - 128), min(2048, c_lo + 640)
        return c_lo, win_lo, win_hi

    def a_make(kind, ti):
        cond = 0 if kind == "p" else 1
        xsrc = xp if kind == "p" else xs
        c_lo, win_lo, win_hi = a_geom(kind, ti)
        WB = (win_hi - win_lo) // 128
        hd = [Dep() for _ in range(32)]
        claim(["H"], hd)
        xbd = [Dep(), Dep()]
        xnd = [Dep(), Dep()]
        claim(["S"], [xbd[0]])
        claim(["Q"], [xbd[1]] + xnd)
        blocks = [dict(n=128, rows=[(win_lo + b * 128, 128, 0)], dst=[(0, 128, b * 128)]) for b in range(WB)]
        lnt = h_steps(xsrc, blocks, hxb, xbd, hxn, xnd, A1[cond], B1[cond], hT, hd, [], split=True)
        return dict(kind=kind, ti=ti, hd=hd, lnt=lnt)

    def pass_a_tile(ctx, epi, nxt):
        kind, ti = ctx["kind"], ctx["ti"]
        cond = 0 if kind == "p" else 1
        xsrc = xp if kind == "p" else xs
        c_lo, win_lo, win_hi = a_geom(kind, ti)
        if kind == "p":
            nseg, L = 2, 256
        else:
            nseg, L = 1, 512
        W = win_hi - win_lo
        WB = W // 128
        cofs = c_lo - win_lo
        hd = ctx["hd"]
        scd = [Dep() for _ in range(16)]
        claim(["S"], scd)

        cw = new_work(8)
        LP = L + 2
        mbuf = [RK[:, 0:516], RK[:, 516:1032]]
        ccs = [RK[:, 1032:1544], RK[:, 1544:2056]]
        tcv = [RK[:, 2056:2568], RK[:, 2568:3080]]
        hrows = []
        if kind == "s":
            if cofs - 1 >= 0:
                hrows.append((cofs - 1, 0))
            if cofs + 512 < W:
                hrows.append((cofs + 512, 513))
        for k in range(2):
            P.op("dve", lambda e, k=k: e.memset(mbuf[k], 0.0), writes=[cw[k]])
        hdep = Dep()
        nhal = len(hrows)

        hThd = Dep()
        alias([hThd], tile_state["hTh"])
        tile_state["hTh"] = [hThd]
        if nhal:
            c0 = hrows[0][0]
            hsrc = hT[:, :, c0:c0 + 514:513] if nhal == 2 else hT[:, :, c0:c0 + 1]
            P.op("dve", lambda e: e.tensor_copy(hTh[:, :, 0:nhal], hsrc), reads=hd, writes=[hThd])
        cn = 0
        for jj in range(8):
            wcc, wccd = wload(wv_in[:, :, I_CC + jj * 256:I_CC + (jj + 1) * 256], ("cc", jj))
            cc_ps = []
            for hh in range(2):
                pt, ptd = nextps()
                pairs = [(wcc[:, kc, hh * 128:(hh + 1) * 128], hT[:, kc, cofs:cofs + 512]) for kc in range(KC)]
                pe_group(pt[:, :], pairs, reads=[wccd] + hd, writes=[ptd])
                hp = None
                if hrows:
                    hp, hpd = nextps()
                    pairs = [(wcc[:, kc, hh * 128:(hh + 1) * 128], hTh[:, kc, 0:nhal]) for kc in range(KC)]
                    pe_group(hp[:, 0:nhal], pairs, reads=[wccd, hThd], writes=[hpd])
                    cc_ps.append((pt, ptd, hp, hpd))
                else:
                    cc_ps.append((pt, ptd, None, None))
            wch, wchd = wload(wv_in[:, :, I_CH + jj * 256:I_CH + (jj + 1) * 256], ("ch", jj))
            mks = []
            for hh in range(2):
                k = cn % 2
                cn += 1
                pt, ptd, hp, hpd = cc_ps[hh]
                act_copy(ccs[k], pt[:, :], [ptd], [cw[2 + k]])
                p2, p2d = nextps()
                pairs = [(wch[:, kc, hh * 128:(hh + 1) * 128], hT[:, kc, cofs:cofs + 512]) for kc in range(KC)]
                pe_group(p2[:, :], pairs, reads=[wchd] + hd, writes=[p2d])
                mb3 = mbuf[k][:, 0:nseg * LP].rearrange("p (s l) -> p s l", s=nseg)
                P.op("dve", lambda e, mb3=mb3, k=k, p2=p2: e.tensor_tensor(
                    mb3[:, :, 1:L + 1], ccs[k].rearrange("p (s l) -> p s l", s=nseg),
                    p2[:, :].rearrange("p (s l) -> p s l", s=nseg), ALU.mult),
                    reads=[cw[2 + k], p2d], writes=[cw[k]])
                if hrows:
                    pairs = [(wch[:, kc, hh * 128:(hh + 1) * 128], hTh[:, kc, 0:nhal]) for kc in range(KC)]
                    pe_group(hp[:, 2:2 + nhal], pairs, reads=[wchd, hThd], writes=[hpd])
                    act_copy(halo4[:, k, :], hp[:, 0:4], [hpd], [hdep])
                    mc0 = hrows[0][1]
                    mdst = mbuf[k][:, mc0:mc0 + 514:513] if nhal == 2 else mbuf[k][:, mc0:mc0 + 1]
                    P.op("dve", lambda e, k=k, mdst=mdst: e.tensor_tensor(
                        mdst, halo4[:, k, 0:nhal], halo4[:, k, 2:2 + nhal], ALU.mult),
                        reads=[hdep], writes=[cw[k]])
                mks.append(k)
            wcb, wcbd = wload(wv_in[:, :, I_CB + jj * 256:I_CB + (jj + 1) * 256], ("cb", jj))
            for hh in range(2):
                j = 2 * jj + hh
                k = mks[hh]
                mb3 = mbuf[k][:, 0:nseg * LP].rearrange("p (s l) -> p s l", s=nseg)
                t3 = tcv[k].rearrange("p (s l) -> p s l", s=nseg)
                wc_ = w_sconvT[:, :].rearrange("p (k c) -> p k c", k=3)
                P.op("dve", lambda e, mb3=mb3, t3=t3, j=j: e.tensor_scalar(t3, mb3[:, :, 1:L + 1], wc_[:, 1, j:j + 1], None, ALU.mult),
                     reads=[cw[k], cd["w_sconvT"]], writes=[cw[4 + k]])
                P.op("dve", lambda e, mb3=mb3, t3=t3, j=j: e.scalar_tensor_tensor(t3, mb3[:, :, 0:L], wc_[:, 0, j:j + 1], t3, ALU.mult, ALU.add),
                     reads=[cw[k]], writes=[cw[4 + k]])
                P.op("dve", lambda e, mb3=mb3, t3=t3, j=j: e.scalar_tensor_tensor(t3, mb3[:, :, 2:L + 2], wc_[:, 2, j:j + 1], t3, ALU.mult, ALU.add),
                     reads=[cw[k]], writes=[cw[4 + k]])
                pt, ptd = nextps()
                pairs = [(wcb[:, kc, hh * 128:(hh + 1) * 128], hT[:, kc, cofs:cofs + 512]) for kc in range(KC)]
                pe_group(pt[:, :], pairs, reads=[wcbd] + hd, writes=[ptd])
                P.op("dve", lambda e, pt=pt, k=k, j=j: e.tensor_tensor(scv[:, j, :], pt[:, :], tcv[k], ALU.mult),
                     reads=[ptd, cw[4 + k]], writes=[scd[j]])
            if jj < len(epi):
                for s in epi[jj]:
                    s()

        qad = [[Dep() for _ in range(4)] for _ in range(32)]
        kTd = [Dep() for _ in range(8)]
        vd = [Dep() for _ in range(6)]
        kcTd = [Dep() for _ in range(8)]
        vcd = Dep()
        mrgd = [Dep() for _ in range(32)]
        claim(["Q"], [d for row in qad for d in row])
        claim(["O"], kTd + vd + kcTd + [vcd])

        ropd = None
        if kind == "s":
            ropd = ropd_g
            cload(ropeC[:, 0:W], ropeC_d[:, win_lo:win_hi], ropd)
            cload(ropeS[:, 0:W], ropeS_d[:, win_lo:win_hi], ropd)

        wk = new_work(9)
        kraw = [RK[:, 0:256].bitcast(BF16), RK[:, 256:512].bitcast(BF16)]
        t1b = [RK[:, 512:1024], RK[:, 1024:1536]]
        t2b = [RK[:, 1536:2048], RK[:, 2048:2560]]
        ropn = [0]

        def evac_fm(psrc, psdep, n, dst, dstdep, wcol0):
            if kind == "p":
                any_copy(dst, psrc, [psdep], [dstdep])
                return
            k = ropn[0] % 2
            ropn[0] += 1
            kr, t1, t2 = kraw[k][:, 0:n], t1b[k][:, 0:n], t2b[k][:, 0:n]
            act_copy(kr, psrc, [psdep], [wk[k]])
            p2, p2d = nextps()
            pe_group(p2[:, 0:n], [(perm_b[:, :], kr)], reads=[wk[k], cd["perm_b"]], writes=[p2d])
            P.op("dve", lambda e: e.tensor_tensor(t1, psrc, ropeC[:, wcol0:wcol0 + n], ALU.mult),
                 reads=[psdep, ropd], writes=[wk[2 + k]])
            P.op("dve", lambda e: e.tensor_tensor(t2, p2[:, 0:n], ropeS[:, wcol0:wcol0 + n], ALU.mult),
                 reads=[p2d, ropd], writes=[wk[4 + k]])
            P.op("dve", lambda e: e.tensor_tensor(dst, t1, t2, ALU.add),
                 reads=[wk[2 + k], wk[4 + k]], writes=[dstdep])

        if W <= 512:
            segs = [(0, W)]
        else:
            segs = [(0, W // 2), (W // 2, W)]

        def kv_out(pt, ptd, dram, row0, col0):
            i = kvn[0] % 4
            kvn[0] += 1
            any_copy(kvst[:, i, :], pt[:, 0:256], [ptd], [kvd[i]])
            ev = P.op("sp", lambda e: e.dma_start(out=dram[row0:row0 + 128, col0:col0 + 256], in_=kvst[:, i, :]),
                      reads=[kvd[i]], dsem=kvsem[i])
            out_events.append(ev)

        for jj in range(4):
            wt, wd = wload(wv_in[:, :, I_K + jj * 256:I_K + (jj + 1) * 256], ("k", jj))
            for hh in range(2):
                g = 2 * jj + hh
                for (s0, s1) in segs:
                    pt, ptd = nextps()
                    pairs = [(wt[:, kc, hh * 128:(hh + 1) * 128], hT[:, kc, s0:s1]) for kc in range(KC)]
                    pe_group(pt[:, 0:s1 - s0], pairs, reads=[wd] + hd, writes=[ptd])
                    evac_fm(pt[:, 0:s1 - s0], ptd, s1 - s0, kT[:, g, s0:s1], kTd[g], s0)
            if kind == "p":
                for b in range(4):
                    pt, ptd = nextps()
                    pairs = [(hT[:, kc, b * 128:(b + 1) * 128], wt[:, kc, :]) for kc in range(KC)]
                    pe_group(pt[:, 0:256], pairs, reads=[wd] + hd, writes=[ptd])
                    kv_out(pt, ptd, nk_o, c_lo + b * 128, jj * 256)
        for jj in range(4):
            wt, wd = wload(wv_in[:, :, I_V + jj * 256:I_V + (jj + 1) * 256], ("v", jj))
            for b in range(WB):
                pt, ptd = nextps()
                pairs = [(hT[:, kc, b * 128:(b + 1) * 128], wt[:, kc, :]) for kc in range(KC)]
                pe_group(pt[:, 0:256], pairs, reads=[wd] + hd, writes=[ptd])
                any_copy(vtok[:, b, jj * 256:(jj + 1) * 256], pt[:, 0:256], [ptd], [vd[b]])
                if kind == "p":
                    kv_out(pt, ptd, nv_o, c_lo + b * 128, jj * 256)

        if kind == "s":
            ckt = RK[:, 2560:4096].bitcast(BF16)
            for kb in range(4):
                sl = kb % 3
                slot = ckt[:, sl * 1024:(sl + 1) * 1024]
                sd_ = wk[6 + sl]
                P.op("pool", lambda g_, slot=slot, kb=kb: g_.dma_start(out=slot, in_=ck[kb * 128:(kb + 1) * 128, :]),
                     writes=[sd_], dsem=cksem[sl])
                for g2 in range(2):
                    tp, tpd = nextps()
                    tpb = psbf(tp)[:, 0:512].rearrange("p (i t) -> p i t", i=4)

                    def tfn(pe, tpb=tpb, slot=slot, g2=g2):
                        for i in range(4):
                            g = g2 * 4 + i
                            ins = pe.transpose(tpb[:, i, :], slot[:, g * 128:(g + 1) * 128], ident_b[:, :])
                        return ins
                    P.op("pe", tfn, reads=[sd_, cd["ident_b"]], writes=[tpd])
                    for i in range(4):
                        g = g2 * 4 + i
                        any_copy(kcT[:, g, kb * 128:(kb + 1) * 128], tpb[:, i, :], [tpd], [kcTd[g]])
            P.op("pool", lambda g_: g_.dma_start(out=vc, in_=cv.rearrange("(b p) n -> p b n", p=128)),
                 writes=[vcd], dsem=cksem[3])

        for jj in range(16):
            wt, wd = wload(wv_in[:, :, jj * 256:(jj + 1) * 256], ("q", jj))
            for hh in range(2):
                h = 2 * jj + hh
                pt, ptd = nextps()
                pairs = [(wt[:, kc, hh * 128:(hh + 1) * 128], hT[:, kc, cofs:cofs + 512]) for kc in range(KC)]
                pe_group(pt[:, :], pairs, reads=[wd] + hd, writes=[ptd])
                if kind == "p":
                    evac_rr[0] += 1
                    if evac_rr[0] % 2:
                        P.op("act", lambda e, h=h, pt=pt: e.mul(qa[:, h, :], pt[:, :], SCALE), reads=[ptd], writes=qad[h])
                    else:
                        P.op("dve", lambda e, h=h, pt=pt: e.tensor_scalar(qa[:, h, :], pt[:, :], SCALE, None, ALU.mult),
                             reads=[ptd], writes=qad[h])
                else:
                    k = ropn[0] % 2
                    ropn[0] += 1
                    kr, t1, t2 = kraw[k], t1b[k], t2b[k]
                    act_copy(kr, pt[:, :], [ptd], [wk[k]])
                    p2, p2d = nextps()
                    pe_group(p2[:, :], [(perm_b[:, :], kr)], reads=[wk[k], cd["perm_b"]], writes=[p2d])
                    P.op("dve", lambda e, t1=t1, pt=pt: e.scalar_tensor_tensor(t1, pt[:, :], SCALE, ropeC[:, cofs:cofs + 512],
                                                                                ALU.mult, ALU.mult),
                         reads=[ptd, ropd], writes=[wk[2 + k]])
                    P.op("dve", lambda e, t2=t2, p2=p2: e.scalar_tensor_tensor(t2, p2[:, :], SCALE, ropeS[:, cofs:cofs + 512],
                                                                                ALU.mult, ALU.mult),
                         reads=[p2d, ropd], writes=[wk[4 + k]])
                    P.op("dve", lambda e, t1=t1, t2=t2, h=h: e.tensor_tensor(qa[:, h, :], t1, t2, ALU.add),
                         reads=[wk[2 + k], wk[4 + k]], writes=qad[h])

        aw = new_work(4)
        p_sb = [RK[:, 0:512].bitcast(BF16), RK[:, 512:1024].bitcast(BF16)]
        pT_sb = [RK[:, 1024:1536].bitcast(BF16), RK[:, 1536:2048].bitcast(BF16)]
        items = []
        if kind == "p":
            for s in range(2):
                for qb in range(2):
                    for h in range(NH):
                        g = h // 4
                        q0 = s * 256 + qb * 128
                        items.append(dict(h=h, g=g, qblk=q0 // 128, q0=q0, k0=s * 256, k1=(s + 1) * 256, mask=None,
                                          vblocks=[(vtok, 2 * s + kb, vd[2 * s + kb]) for kb in range(2)], ctx=False))
        else:
            for qb in range(4):
                gb = c_lo // 128 + qb
                wb = gb - win_lo // 128
                has_l, has_r = gb - 1 >= 0, gb + 1 < 16
                b0 = wb - 1 if has_l else wb
                b1 = wb + 1 if has_r else wb
                m0 = 0 if has_l else 128
                m1 = 384 if has_r else 256
                for h in range(NH):
                    g = h // 4
                    items.append(dict(h=h, g=g, qblk=qb, q0=qb * 128, k0=b0 * 128, k1=(b1 + 1) * 128, mask=(m0, m1),
                                      vblocks=[(vtok, b, vd[b]) for b in range(b0, b1 + 1)], ctx=True))
        n_it = len(items)
        st_d = [Dep() for _ in range(n_it)]

        def step1(i):
            it = items[i]
            k = i % 2
            h, g, q0 = it["h"], it["g"], it["q0"]
            nlat = it["k1"] - it["k0"]
            nk = nlat + (512 if it["ctx"] else 0)
            it["nk"] = nk
            it["nlat"] = nlat
            pd_ = pdb[k]
            d0, d1 = psd[2 * k], psd[2 * k + 1]
            cs = 512 - nlat - 1
            rd = [qad[h][it["qblk"]], kTd[g], cd["ones_f"], cd["sinkb"]]
            wr = [d0]
            if it["mask"] is not None:
                rd += [cd["mask_b"], cd["ident_b"]]
            if it["ctx"]:
                rd.append(kcTd[g])
                wr.append(d1)

            def fn(pe):
                has_m = it["mask"] is not None
                pe.matmul(pd_[:, cs:cs + 1], ones_f[0:1, :], sinkb[0:1, h:h + 1], start=True, stop=False,
                          skip_group_check=True)
                ins = pe.matmul(pd_[:, 512 - nlat:512], qa[:, h, q0:q0 + 128], kT[:, g, it["k0"]:it["k1"]],
                                start=False, stop=not has_m, skip_group_check=True)
                if has_m:
                    m0, m1 = it["mask"]
                    ins = pe.matmul(pd_[:, 512 - nlat:512], ident_b[:, :], mask_b[:, m0:m1], start=False, stop=True,
                                    skip_group_check=True)
                if it["ctx"]:
                    ins = pe.matmul(pd_[:, 512:1024], qa[:, h, q0:q0 + 128], kcT[:, g, :], start=True, stop=True)
                return ins
            P.op("pe", fn, reads=rd, writes=wr)
            sd = st_d[i]
            c0 = (i % 8) * 8
            stv = astat[:, c0:c0 + 8]
            alias([sd], [st_d[i - 8]] if i >= 8 else [])
            sall = pd_[:, cs:cs + 1 + nk]
            P.op("dve", lambda e: e.reduce_max(stv[:, 0:1], sall, axis=AX.X, negate=True), reads=wr, writes=[sd])
            pdep = aw[k]
            P.op("act", lambda e: e.activation(p_sb[k][:, 1:2 + nk], sall, AF.Exp, bias=stv[:, 0:1], scale=1.0,
                                               accum_out=stv[:, 1:2]),
                 reads=wr + [sd], writes=[pdep, sd])
            P.op("dve", lambda e: e.reciprocal(stv[:, 2:3], stv[:, 1:2]), reads=[sd], writes=[sd])
            P.op("dve", lambda e: e.tensor_scalar(p_sb[k][:, 2:2 + nk], p_sb[k][:, 2:2 + nk], stv[:, 2:3], None, ALU.mult),
                 reads=[sd], writes=[pdep])

        def step2(i):
            it = items[i]
            k = i % 2
            nk = it["nk"]
            nb = nk // 128
            tp, tpd = ps[4 + k], psd[4 + k]
            tpb = psbf(tp)

            def fn(pe):
                for j in range(nb):
                    ins = pe.transpose(tpb[:, j * 128:(j + 1) * 128], p_sb[k][:, 2 + j * 128:2 + (j + 1) * 128], ident_b[:, :])
                return ins
            P.op("pe", fn, reads=[aw[k], cd["ident_b"]], writes=[tpd])
            act_copy(pT_sb[k][:, 0:nk], tpb[:, 0:nk], [tpd], [aw[2 + k]])

        def step3(i):
            it = items[i]
            k = i % 2
            h, g, q0 = it["h"], it["g"], it["q0"]
            po, pod = ps[6 + k], psd[6 + k]
            pairs = []
            rd = [aw[2 + k]]
            j = 0
            for (vt, b, dep) in it["vblocks"]:
                pairs.append((vt[:, b, g * 128:(g + 1) * 128], pT_sb[k][:, j * 128:(j + 1) * 128]))
                rd.append(dep)
                j += 1
            if it["ctx"]:
                for kb in range(4):
                    pairs.append((vc[:, kb, g * 128:(g + 1) * 128], pT_sb[k][:, j * 128:(j + 1) * 128]))
                    j += 1
                rd.append(vcd)
            pe_group(po[:, 0:128], pairs, reads=rd, writes=[pod])
            dve_copy(qa[:, h, q0:q0 + 128], po[:, 0:128], [pod], [qad[h][it["qblk"]]])

        for i in range(n_it + 2):
            if i < n_it:
                step1(i)
            if 0 <= i - 1 < n_it:
                step2(i - 1)
            if 0 <= i - 2 < n_it:
                step3(i - 2)
        qflat = [d for row in qad for d in row]

        claim(["O"], mrgd)
        gw = new_work(8)
        sga = [RK[:, i * 512:(i + 1) * 512] for i in range(4)]
        sgc = [RK[:, (4 + i) * 512:(5 + i) * 512] for i in range(4)]
        for jj in range(16):
            ks = [(2 * jj + hh) % 4 for hh in range(2)]
            wt, wd = wload(wv_in[:, :, I_GA + jj * 256:I_GA + (jj + 1) * 256], ("ga", jj))
            for hh in range(2):
                k = ks[hh]
                pt, ptd = nextps()
                pairs = [(wt[:, kc, hh * 128:(hh + 1) * 128], hT[:, kc, cofs:cofs + 512]) for kc in range(KC)]
                pe_group(pt[:, :], pairs, reads=[wd] + hd, writes=[ptd])
                P.op("act", lambda e, k=k, pt=pt: e.activation(sga[k], pt[:, :], AF.Sigmoid), reads=[ptd], writes=[gw[k]])
            wt, wd = wload(wv_ao[:, :, jj * 256:(jj + 1) * 256], ("ao", jj))
            for hh in range(2):
                k = ks[hh]
                pt, ptd = nextps()
                pairs = [(wt[:, kc, hh * 128:(hh + 1) * 128], qa[:, kc, :]) for kc in range(KC)]
                pe_group(pt[:, :], pairs, reads=[wd] + qflat, writes=[ptd])
                P.op("dve", lambda e, k=k, pt=pt: e.tensor_tensor(sga[k], pt[:, :], sga[k], ALU.mult), reads=[ptd, gw[k]], writes=[gw[k]])
            wt, wd = wload(wv_in[:, :, I_GC + jj * 256:I_GC + (jj + 1) * 256], ("gc", jj))
            for hh in range(2):
                k = ks[hh]
                pt, ptd = nextps()
                pairs = [(wt[:, kc, hh * 128:(hh + 1) * 128], hT[:, kc, cofs:cofs + 512]) for kc in range(KC)]
                pe_group(pt[:, :], pairs, reads=[wd] + hd, writes=[ptd])
                P.op("act", lambda e, k=k, pt=pt: e.activation(sgc[k], pt[:, :], AF.Sigmoid), reads=[ptd], writes=[gw[4 + k]])
            wt, wd = wload(wv_co[:, :, jj * 256:(jj + 1) * 256], ("co", jj))
            for hh in range(2):
                k = ks[hh]
                j = 2 * jj + hh
                pt, ptd = nextps()
                pairs = [(wt[:, kc, hh * 128:(hh + 1) * 128], scv[:, kc, :]) for kc in range(16)]
                pe_group(pt[:, :], pairs, reads=[wd] + scd, writes=[ptd])
                P.op("dve", lambda e, k=k, pt=pt: e.tensor_tensor(sgc[k], pt[:, :], sgc[k], ALU.mult), reads=[ptd, gw[4 + k]], writes=[gw[4 + k]])
                P.op("dve", lambda e, k=k, j=j: e.tensor_tensor(merged[:, j, :], sga[k], sgc[k], ALU.add),
                     reads=[gw[k], gw[4 + k]], writes=[mrgd[j]])

        nctx = a_make(*nxt) if nxt is not None else None
        groups = {}
        if nctx is not None:
            for bi, (fL, fN, fT) in enumerate(nctx["lnt"]):
                groups.setdefault(bi, []).append((1, fL))
                groups.setdefault(bi + 2, []).append((0, fN))
                groups.setdefault(bi + 3, []).append((2, fT))
            assert max(groups) < 16
        fstd_a = [[Dep() for _ in range(4)] for _ in range(2)]
        claim(["O"], [d for row in fstd_a for d in row])
        for nt in range(16):
            wt, wd = wload(wv_mix[:, :, nt * 256:(nt + 1) * 256], ("mix", nt))
            k = nt % 2
            for b in range(4):
                pt, ptd = nextps()
                pairs = [(merged[:, kc, b * 128:(b + 1) * 128], wt[:, kc, :]) for kc in range(KC)]
                pe_group(pt[:, 0:256], pairs, reads=[wd] + mrgd, writes=[ptd])
                any_copy(fstA[k][:, b, :], pt[:, 0:256], [ptd], [fstd_a[k][b]])
                P.op("act", lambda e, pt=pt, b=b, nt=nt: e.activation(junk[:, :], pt[:, 0:256], AF.Square,
                                                                       accum_out=stat2[:, b, nt:nt + 1]),
                     reads=[ptd], writes=[jdB, s2dB[b]])
            P.op("sp", lambda e, k=k, nt=nt: e.dma_start(out=fscr_v[:, :, nt * 256:(nt + 1) * 256], in_=fstA[k]),
                 reads=fstd_a[k], writes=[fscrd], dsem=fssem)
            for (_, s) in sorted(groups.get(nt, []), key=lambda t: t[0]):
                s()
        EFd_, EXd_, EGd_ = Dep(), [Dep(), Dep()], Dep()
        claim(["Q"], [EFd_, EXd_[0]])
        claim(["O"], [EXd_[1], EGd_])
        nepi = epi_steps(EFa, EXa, EGa, EFd_, EXd_, EGd_, cond, c_lo, xsrc, None,
                         x1p if kind == "p" else x1s, x1p_d if kind == "p" else x1s_d, False)
        return nepi, nctx

    def final_epilogue(kind, cond, sub, c_lo, xsrc, fbuf, fdeps, s2d, xbufs, xb0_old, xb1_old, Grow, grow_old,
                       dst, dst_deps, is_out, src_deps=None):
        gd = Dep()
        alias([gd], grow_old)
        r = 2 * sub + cond
        P.op("sp", lambda e: e.dma_start(out=Grow, in_=grow[r:r + 1, :].partition_broadcast(128)),
             reads=[growd], writes=[gd], dsem=gsem)
        xbd = [Dep(), Dep()]
        alias([xbd[0]], xb0_old)
        alias([xbd[1]], xb1_old)
        sd = Dep()
        for b in range(4):
            k = b % 2
            xb = xbufs[k]
            gblk = (c_lo // 128) + b
            rd = [src_deps[gblk]] if src_deps is not None else []
            P.op("sp", lambda e, xb=xb, b=b: e.dma_start(out=xb, in_=xsrc[c_lo + b * 128:c_lo + (b + 1) * 128, :]),
                 reads=rd, writes=[xbd[k]], dsem=xsem[k])
            c0 = 8 + b * 4
            P.op("dve", lambda e, b=b, c0=c0: e.reduce_sum(stat[:, c0:c0 + 1], stat2[:, b, :], axis=AX.X),
                 reads=[s2d[b]], writes=[sd])
            P.op("act", lambda e, c0=c0: e.activation(stat[:, c0 + 1:c0 + 2], stat[:, c0:c0 + 1], AF.Sqrt, bias=EPS, scale=1.0 / D),
                 reads=[sd], writes=[sd])
            P.op("dve", lambda e, c0=c0: e.reciprocal(stat[:, c0 + 2:c0 + 3], stat[:, c0 + 1:c0 + 2]), reads=[sd], writes=[sd])
            if fbuf.dtype == F32:
                P.op("dve", lambda e, b=b: e.tensor_tensor(fbuf[:, b, :], fbuf[:, b, :], Grow, ALU.mult),
                     reads=[gd], writes=[fdeps[b]])
                P.op("dve", lambda e, b=b, xb=xb, c0=c0: e.scalar_tensor_tensor(xb, fbuf[:, b, :], stat[:, c0 + 2:c0 + 3], xb,
                                                                                 ALU.mult, ALU.add),
                     reads=[fdeps[b], sd], writes=[xbd[k]])
            else:
                tb = epi_tmp[0]
                P.op("dve", lambda e, b=b, c0=c0, tb=tb: e.scalar_tensor_tensor(tb, fbuf[:, b, :], stat[:, c0 + 2:c0 + 3], Grow,
                                                                                 ALU.mult, ALU.mult),
                     reads=[fdeps[b], sd, gd], writes=[epi_tmp[1]])
                P.op("dve", lambda e, xb=xb, tb=tb: e.tensor_tensor(xb, xb, tb, ALU.add),
                     reads=[epi_tmp[1]], writes=[xbd[k]])
            ev = P.op("sp", lambda e, xb=xb, b=b: e.dma_start(out=dst[c_lo + b * 128:c_lo + (b + 1) * 128, :], in_=xb),
                      reads=[xbd[k]], writes=[dst_deps[gblk]] if not is_out else [], dsem=osem[k])
            if is_out:
                out_events.append(ev)
        tile_state["ep_deps"] = [gd, xbd[0], xbd[1], sd]

    tile_state = {"deps": [], "ep_deps": [], "hTh": []}
    epi_tmp = [None, None]
    cksem = [P.dma_sem("ck") for _ in range(4)]

    tiles_a = [("p", ti) for ti in range(n_ptiles)] + [("s", ti) for ti in range(n_stiles)]
    actx = a_make(*tiles_a[0])
    for (fL, fN, fT) in actx["lnt"]:
        fL()
        fN()
        fT()
    aepi = []
    for idx in range(len(tiles_a)):
        preconv["active"] = idx >= 1
        defer["on"] = (idx == 0 and len(tiles_a) > 1)
        aepi, actx = pass_a_tile(actx, aepi, tiles_a[idx + 1] if idx + 1 < len(tiles_a) else None)
    for grp in aepi:
        for s in grp:
            s()
    preconv["active"] = False
    for sd_ in preconv["deps"]:
        sd_.w = (pcsem[0], pcsem[1])

    XN2 = rview(60480, 4096)
    FSTf = rview(64576, 4096).bitcast(F32)
    fst = [FSTf[:, k * 1024:(k + 1) * 1024].rearrange("p (b n) -> p b n", b=4) for k in range(2)]
    E_CH0 = 22
    EF = rview(11264, 8192).bitcast(F32)
    EX = [rview(19456, 8192).bitcast(F32), rview(27648, 8192).bitcast(F32)]
    EG = rview(35840, 8192).bitcast(F32)
    xbB = RK[:, 0:4096]
    h2d = [Dep() for _ in range(32)]
    actd = [Dep() for _ in range(FC)]
    xn2d = Dep()
    fstd = [[Dep() for _ in range(4)] for _ in range(2)]
    EFd, EXd, EGd = Dep(), [Dep(), Dep()], Dep()
    hThB = Dep()
    alias(h2d + actd + [xn2d, EFd, EGd] + EXd + [d for row in fstd for d in row],
          reg_users["H"] + reg_users["S"] + reg_users["Q"] + reg_users["O"])
    alias([hThB], tile_state["hTh"])

    def b_make(kind, ti):
        cond = 0 if kind == "p" else 1
        x1 = x1p if kind == "p" else x1s
        x1d = x1p_d if kind == "p" else x1s_d
        c_lo = 512 * ti
        hr = []
        if kind == "s":
            if c_lo - 1 >= 0:
                hr.append((c_lo - 1, 0))
            if c_lo + 512 < 2048:
                hr.append((c_lo + 512, 513))
        gb0 = c_lo // 128
        blocks = [dict(n=128, rows=[(c_lo + b * 128, 128, 0)], dst=[(0, 128, 1 + b * 128)], srcdeps=[x1d[gb0 + b]])
                  for b in range(4)]
        if hr:
            rows = [(row, 1, idx) for idx, (row, col) in enumerate(hr)]
            dst = [(idx, 1, col) for idx, (row, col) in enumerate(hr)]
            sdeps = [x1d[row // 128] for (row, col) in hr]
            blocks.append(dict(n=len(hr), rows=rows, dst=dst, srcdeps=sdeps))
        xbd = new_work(1)[0]
        steps = h_steps(x1, blocks, xbB, xbd, XN2, xn2d, A2[cond], B2[cond], hT2, h2d, [])

        def hth():
            if hr:
                nh0 = len(hr)
                col00 = hr[0][1]
                hsrc = hT2[:, :, 0:514:513] if nh0 == 2 else hT2[:, :, col00:col00 + 1]
                P.op("dve", lambda e: e.tensor_copy(hTh[:, :, 0:nh0], hsrc), reads=h2d, writes=[hThB])
        steps.append(hth)
        return dict(kind=kind, ti=ti, cond=cond, x1=x1, x1d=x1d, c_lo=c_lo, hr=hr, hsteps=steps,
                    ydst=yp if kind == "p" else ys)

    def b_up(ctx, epi):
        kind, hr = ctx["kind"], ctx["hr"]
        nseg, L = (2, 256) if kind == "p" else (1, 512)
        LP = L + 2
        assert len(epi) <= E_CH0 // 2
        uw = new_work(6)
        gbuf = [RK[:, 0:516], RK[:, 516:1032]]
        tbf = [RK[:, 1032:1544], RK[:, 1544:2056]]
        for k in range(2):
            P.op("dve", lambda e, k=k: e.memset(gbuf[k], 0.0), writes=[uw[k]])
        wfc = w_ffn_convT[:, :].rearrange("p (k c) -> p k c", k=3)
        cn = 0
        for jj in range(FC // 2):
            wt, wd = wload(wv_up[:, :, jj * 256:(jj + 1) * 256], ("gt", jj))
            ks = []
            for hh in range(2):
                j = 2 * jj + hh
                k = cn % 2
                cn += 1
                ks.append(k)
                pt, ptd = nextps()
                pairs = [(wt[:, kc, hh * 128:(hh + 1) * 128], hT2[:, kc, 1:513]) for kc in range(KC)]
                pe_group(pt[:, :], pairs, reads=[wd] + h2d, writes=[ptd])
                gb3 = gbuf[k][:, 0:nseg * LP].rearrange("p (s l) -> p s l", s=nseg)
                act_copy(gb3[:, :, 1:L + 1], pt[:, :].rearrange("p (s l) -> p s l", s=nseg), [ptd], [uw[k]])
                if hr:
                    hp, hpd = nextps()
                    nh_ = len(hr)
                    col0 = hr[0][1]
                    pairs = [(wt[:, kc, hh * 128:(hh + 1) * 128], hTh[:, kc, 0:nh_]) for kc in range(KC)]
                    pe_group(hp[:, 0:nh_], pairs, reads=[wd, hThB], writes=[hpd])
                    gdst = gbuf[k][:, 0:514:513] if nh_ == 2 else gbuf[k][:, col0:col0 + 1]
                    dve_copy(gdst, hp[:, 0:nh_], [hpd], [uw[k]])
                t3 = tbf[k].rearrange("p (s l) -> p s l", s=nseg)
                P.op("dve", lambda e, gb3=gb3, t3=t3, j=j: e.tensor_scalar(t3, gb3[:, :, 1:L + 1], wfc[:, 1, j:j + 1], None, ALU.mult),
                     reads=[uw[k], cd["w_ffn_convT"]], writes=[uw[2 + k]])
                P.op("dve", lambda e, gb3=gb3, t3=t3, j=j: e.scalar_tensor_tensor(t3, gb3[:, :, 0:L], wfc[:, 0, j:j + 1], t3, ALU.mult, ALU.add),
                     reads=[uw[k]], writes=[uw[2 + k]])
                P.op("dve", lambda e, gb3=gb3, t3=t3, j=j: e.scalar_tensor_tensor(t3, gb3[:, :, 2:L + 2], wfc[:, 2, j:j + 1], t3, ALU.mult, ALU.add),
                     reads=[uw[k]], writes=[uw[2 + k]])
                P.op("act", lambda e, k=k: e.activation(tbf[k], tbf[k], AF.Silu), reads=[uw[2 + k]], writes=[uw[2 + k]])
            wt, wd = wload(wv_up[:, :, DFF + jj * 256:DFF + (jj + 1) * 256], ("val", jj))
            for hh in range(2):
                j = 2 * jj + hh
                k = ks[hh]
                pt, ptd = nextps()
                pairs = [(wt[:, kc, hh * 128:(hh + 1) * 128], hT2[:, kc, 1:513]) for kc in range(KC)]
                pe_group(pt[:, :], pairs, reads=[wd] + h2d, writes=[ptd])
                P.op("dve", lambda e, pt=pt, k=k, j=j: e.tensor_tensor(actv[:, j, :], pt[:, :], tbf[k], ALU.mult),
                     reads=[ptd, uw[2 + k]], writes=[actd[j]])
            if jj < len(epi):
                for s in epi[jj]:
                    s()

    def b_down(ctx, hsteps):
        kparts = [(0, 32), (32, 64), (64, FC)]
        assert len(hsteps) <= 16
        for nt in range(16):
            pts = []
            for b in range(4):
                pt, ptd = nextps()
                pts.append((pt[:, 0:256], ptd))
            for kp, (k0, k1) in enumerate(kparts):
                wt, wd = wload(wv_down[:, k0:k1, nt * 256:(nt + 1) * 256], ("dn", nt, kp))
                for b in range(4):
                    o, od = pts[b]

                    def fn(pe, o=o, wt=wt, b=b, k0=k0, k1=k1, kp=kp):
                        for kc in range(k0, k1):
                            ins = pe.matmul(o, actv[:, kc, b * 128:(b + 1) * 128], wt[:, kc - k0, :],
                                            start=(kp == 0 and kc == k0), stop=(kp == 2 and kc == k1 - 1))
                        return ins
                    P.op("pe", fn, reads=[wd] + actd[k0:k1], writes=[od])
            k = nt % 2
            for b in range(4):
                o, od = pts[b]
                any_copy(fst[k][:, b, :], o, [od], [fstd[k][b]])
                P.op("act", lambda e, o=o, b=b, nt=nt: e.activation(junk[:, :], o, AF.Square, accum_out=stat2[:, b, nt:nt + 1]),
                     reads=[od], writes=[jdB, s2dB[b]])
            P.op("sp", lambda e, k=k, nt=nt: e.dma_start(out=fscr_v[:, :, nt * 256:(nt + 1) * 256], in_=fst[k]),
                 reads=fstd[k], writes=[fscrd], dsem=fssem)
            if nt < len(hsteps):
                hsteps[nt]()

    def b_epi(ctx):
        alias([EGd, EFd, EXd[0], EXd[1]], actd[E_CH0:])
        groups = epi_steps(EF, EX, EG, EFd, EXd, EGd, 2 + ctx["cond"], ctx["c_lo"], ctx["x1"], ctx["x1d"],
                           ctx["ydst"], None, True)
        groups[-1].append(lambda: alias(actd[E_CH0:], [EGd, EFd, EXd[0], EXd[1]]))
        return groups

    tiles_b = [("p", ti) for ti in range(n_ptiles)] + [("s", ti) for ti in range(n_stiles)]
    if do_b and tiles_b:
        ctx = b_make(*tiles_b[0])
        for s in ctx["hsteps"]:
            s()
        epi = []
        for idx in range(len(tiles_b)):
            b_up(ctx, epi)
            nctx = b_make(*tiles_b[idx + 1]) if idx + 1 < len(tiles_b) else None
            b_down(ctx, nctx["hsteps"] if nctx else [])
            epi = b_epi(ctx)
            ctx = nctx
        for grp in epi:
            for s in grp:
                s()

    fin = {}
    for ev in out_events:
        k = id(ev[0])
        if k not in fin or fin[k][1] < ev[1]:
            fin[k] = ev
    for d in x1p_d + x1s_d:
        if d.w is not None:
            k = id(d.w[0])
            if k not in fin or fin[k][1] < d.w[1]:
                fin[k] = d.w
    final_waits = list(fin.values())

    with nc.Block() as block:
        @block.tensor
        def _(e):
            P.emit("pe", e)

        @block.scalar
        def _(e):
            P.emit("act", e)

        @block.vector
        def _(e):
            P.emit("dve", e)

        @block.gpsimd
        def _(e):
            P.emit("pool", e)

        @block.sync
        def _(e):
            P.emit("sp", e)
            for (s_, v_) in final_waits:
                e.wait_ge(s_, v_)
    stack.close()
    return nc


def _consts():
    ident = np.eye(128, dtype=np.float32)
    perm = np.zeros((128, 128), np.float32)
    for k in range(128):
        perm[k, k ^ 32] = 1.0
    i = np.arange(128)[:, None]
    j = np.arange(128)[None, :]
    mL = np.where(j >= i, 0.0, NEGBIG).astype(np.float32)
    mR = np.where(j <= i, 0.0, NEGBIG).astype(np.float32)
    mask = np.concatenate([mL, np.zeros((128, 128), np.float32), mR], axis=1)
    L = 2048
    t = np.arange(L)
    row = (t // 64).astype(np.float32)
    col = (t % 64).astype(np.float32)
    nf = 32
    inv = (10000.0 ** (-np.arange(nf, dtype=np.float32) / nf)).astype(np.float32)
    ang = np.concatenate([row[:, None] * inv, col[:, None] * inv], axis=-1).astype(np.float32)
    cos, sin = np.cos(ang), np.sin(ang)
    C = np.zeros((128, L), np.float32)
    S = np.zeros((128, L), np.float32)
    for d in range(128):
        axis, half, f = d // 64, (d % 64) // 32, d % 32
        C[d] = cos[:, axis * 32 + f]
        S[d] = (-sin[:, axis * 32 + f]) if half == 0 else sin[:, axis * 32 + f]
    return ident, perm, mask, C, S


def _fm(v, nchunk):
    return np.ascontiguousarray(v.reshape(nchunk, 128).T)


_NC_CACHE = {}


def make_in_maps(x_prompt, x_sample, cache_k, cache_v, c, c_ctx, w_mod, b_mod, g_pre_mix, w_in, w_sconv, attn_sink,
                 w_attn_o, w_conv_o, w_mix_out, g_post_mix, g_pre_ffn, w_ffn_up, w_ffn_conv, w_ffn_down, g_post_ffn,
                 cores=range(8)):
    f = lambda a: np.ascontiguousarray(np.asarray(a, dtype=np.float32))
    ident, perm, mask, C, S = _consts()
    gvec = np.concatenate([_fm(f(g)[0], 32) for g in (g_pre_mix, g_post_mix, g_pre_ffn, g_post_ffn)], axis=1)
    shared = {
        "w_mod": f(w_mod)[0], "b_modT": _fm(f(b_mod)[0], 192), "gvec": np.ascontiguousarray(gvec),
        "w_in": f(w_in)[0],
        "w_sconvT": np.ascontiguousarray(np.concatenate([_fm(f(w_sconv)[0, k], 16) for k in range(3)], axis=1)),
        "sinkb": np.ascontiguousarray(np.broadcast_to(f(attn_sink)[0][None, :], (128, 32))),
        "w_attn_o": f(w_attn_o)[0], "w_conv_o": f(w_conv_o)[0], "w_mix_out": f(w_mix_out)[0],
        "w_ffn_up": f(w_ffn_up)[0],
        "w_ffn_convT": np.ascontiguousarray(np.concatenate([_fm(f(w_ffn_conv)[0, k], FC) for k in range(3)], axis=1)),
        "w_ffn_down": f(w_ffn_down)[0],
        "ident": ident, "perm": perm, "maskc": mask, "ropeC": C, "ropeS": S,
    }
    xpf, xsf, ckf, cvf, cf, cctx = f(x_prompt), f(x_sample), f(cache_k), f(cache_v), f(c), f(c_ctx)
    in_maps = []
    for i in cores:
        cond = np.stack([cctx, cf[i]], axis=0)
        condT = np.ascontiguousarray(cond.reshape(2, 32, 128).transpose(2, 1, 0).reshape(128, 64))
        m = dict(shared)
        m["xp"] = np.ascontiguousarray(xpf[4 * i:4 * i + 4].reshape(1024, D))
        m["xs"] = np.ascontiguousarray(xsf[i])
        m["ck"] = np.ascontiguousarray(ckf[i, 0].reshape(PAST, 1024))
        m["cv"] = np.ascontiguousarray(cvf[i, 0].reshape(PAST, 1024))
        m["condT"] = condT
        in_maps.append(m)
    return in_maps


def kernel(**inputs):
    in_maps = make_in_maps(**inputs)
    if "nc" not in _NC_CACHE:
        _NC_CACHE["nc"] = build_program()
    nc = _NC_CACHE["nc"]
    res = run_bass_kernel_spmd(nc, in_maps, core_ids=list(range(8)))
    r = res.results
    y_prompt = np.concatenate([r[i]["yp"].reshape(4, 256, D) for i in range(8)], axis=0)
    y_sample = np.stack([r[i]["ys"] for i in range(8)], axis=0)
    new_k = np.concatenate([r[i]["nk"].reshape(4, 1, 256, 8, 128) for i in range(8)], axis=0)
    new_v = np.concatenate([r[i]["nv"].reshape(4, 1, 256, 8, 128) for i in range(8)], axis=0)
    return (y_prompt.astype(np.float32), y_sample.astype(np.float32), new_k.astype(np.float32), new_v.astype(np.float32))
```

```python
import os
from contextlib import ExitStack
import numpy as np
import concourse.bass as bass
import concourse.mybir as mybir
from concourse.bass_utils import run_bass_kernel_spmd

F32 = mybir.dt.float32
BF16 = mybir.dt.bfloat16
AF = mybir.ActivationFunctionType
ALU = mybir.AluOpType
AX = mybir.AxisListType

D = 4096
KC = 32
DFF = 11008
FC = 86
NH = 32
NKV = 8
PAST = 512
EPS = 1e-6
SCALE = 128.0 ** -0.5
I_K, I_V, I_CB, I_CC, I_CH, I_GA, I_GC = 4096, 5120, 6144, 8192, 10240, 12288, 16384
NEGBIG = -1e30
PRECONV_EVERY = int(os.environ.get('PRECONV_EVERY', '5'))


class Dep:
    __slots__ = ("w", "r", "excl")

    def __init__(self, excl=False):
        self.w = None
        self.r = {}
        self.excl = excl


def alias(new_deps, old_deps):
    acc = {}
    for d in old_deps:
        evs = list(d.r.values())
        if d.w is not None:
            evs.append(d.w)
        for e in evs:
            k = id(e[0])
            if k not in acc or acc[k][1] < e[1]:
                acc[k] = e
    for d in new_deps:
        for k, e in acc.items():
            if k not in d.r or d.r[k][1] < e[1]:
                d.r[k] = e


class Planner:
    def __init__(self, nc, stack):
        self.nc = nc
        self.stack = stack
        self.streams = {"pe": [], "act": [], "dve": [], "pool": [], "sp": []}
        self.esem = {}
        self.nsem = 0
        self.owner = {}

    def new_sem(self, name, owner=None):
        s = self.stack.enter_context(self.nc.semaphore(f"{name}{self.nsem}"))
        self.nsem += 1
        self.owner[id(s)] = owner
        return s

    def dma_sem(self, name):
        return [self.new_sem(name), 0]

    def _eng_event(self, eng):
        st = self.esem.get(eng)
        if st is None or st[1] >= 12000:
            st = [self.new_sem(eng, owner=eng), 0]
            self.esem[eng] = st
        st[1] += 1
        return (st[0], st[1])

    def op(self, eng, fn, reads=(), writes=(), dsem=None, extra=()):
        waits = {}

        def add(ev):
            if ev is None:
                return
            k = id(ev[0])
            if k not in waits or waits[k][1] < ev[1]:
                waits[k] = ev

        ex = [d for d in reads if d.excl]
        if ex:
            reads = [d for d in reads if not d.excl]
            writes = list(writes) + ex
        for d in reads:
            add(d.w)
        for d in writes:
            add(d.w)
            for e in d.r.values():
                add(e)
        for e in extra:
            add(e)
        if dsem is not None:
            dsem[1] += 16
            ev = (dsem[0], dsem[1])
            amt = 16
        else:
            ev = self._eng_event(eng)
            amt = 1
        k = id(ev[0])
        for d in reads:
            if k not in d.r or d.r[k][1] < ev[1]:
                d.r[k] = ev
        for d in writes:
            d.w = ev
            d.r = {}
        self.streams[eng].append((fn, list(waits.values()), ev, amt))
        return ev

    def emit(self, name, eng):
        waited = {}
        for fn, waits, ev, amt in self.streams[name]:
            for (s, v) in waits:
                k = id(s)
                if waited.get(k, 0) >= v:
                    continue
                if name == "pe" and self.owner.get(k) == "pe":
                    continue
                eng.wait_ge(s, v)
                waited[k] = v
            ins = fn(eng)
            ins.then_inc(ev[0], amt)


def build_program(n_ptiles=2, n_stiles=4, do_b=True):
    nc = bass.Bass("TRN2", target_bir_lowering=False)
    stack = ExitStack()
    P = Planner(nc, stack)

    def din(name, shape):
        return nc.dram_tensor(name, shape, F32, kind="ExternalInput").ap()

    def dout(name, shape):
        return nc.dram_tensor(name, shape, F32, kind="ExternalOutput").ap()

    xp = din("xp", [1024, D])
    xs = din("xs", [2048, D])
    ck = din("ck", [PAST, 1024])
    cv = din("cv", [PAST, 1024])
    condT_d = din("condT", [128, 64])
    w_mod = din("w_mod", [D, 6 * D])
    b_modT_d = din("b_modT", [128, 192])
    gvec_d = din("gvec", [128, 128])
    w_in = din("w_in", [D, 20480])
    w_sconvT_d = din("w_sconvT", [128, 48])
    sinkb_d = din("sinkb", [128, 32])
    w_attn_o = din("w_attn_o", [D, D])
    w_conv_o = din("w_conv_o", [2048, D])
    w_mix_out = din("w_mix_out", [D, D])
    w_ffn_up = din("w_ffn_up", [D, 2 * DFF])
    w_ffn_convT_d = din("w_ffn_convT", [128, 3 * FC])
    w_ffn_down = din("w_ffn_down", [DFF, D])
    ident_d = din("ident", [128, 128])
    perm_d = din("perm", [128, 128])
    mask_d = din("maskc", [128, 384])
    ropeC_d = din("ropeC", [128, 2048])
    ropeS_d = din("ropeS", [128, 2048])

    yp = dout("yp", [1024, D])
    ys = dout("ys", [2048, D])
    nk_o = dout("nk", [1024, 1024])
    nv_o = dout("nv", [1024, 1024])

    NSCR = 288
    wscr_t = [nc.dram_tensor(f"wscr{i}", [96, 128, 8192], BF16, kind="Internal").ap() for i in range(3)]
    x1p = nc.dram_tensor("x1p", [1024, D], F32, kind="Internal").ap()
    x1s = nc.dram_tensor("x1s", [2048, D], F32, kind="Internal").ap()
    grow = nc.dram_tensor("grow", [4, D], F32, kind="Internal").ap()
    fscr = nc.dram_tensor("fscr", [512, D], F32, kind="Internal").ap()

    def wview(w):
        return w.rearrange("(kc p) n -> p kc n", p=128)

    wv_mod, wv_in, wv_ao, wv_co, wv_mix, wv_up = (
        wview(w_mod), wview(w_in), wview(w_attn_o), wview(w_conv_o), wview(w_mix_out), wview(w_ffn_up))
    wv_down = wview(w_ffn_down)

    def sb(name, shape, dt):
        return stack.enter_context(nc.sbuf_tensor(name, shape, dt))

    R = sb("R", [128, 69632], BF16)
    RW = [sb(f"RW{i}", [128, 32, 256], BF16) for i in range(2)]
    RK = sb("RK", [128, 4096], F32)
    ident_f = sb("ident_f", [128, 128], F32)
    ident_b = sb("ident_b", [128, 128], BF16)
    perm_f = sb("perm_f", [128, 128], F32)
    perm_b = sb("perm_b", [128, 128], BF16)
    mask_sb = sb("mask_sb", [128, 384], F32)
    ropeC = sb("ropeC_sb", [128, 768], F32)
    ropeS = sb("ropeS_sb", [128, 768], F32)
    condT = sb("condT_sb", [128, 64], F32)
    scT = sb("scT", [128, 64], BF16)
    b_modT = sb("b_modT_sb", [128, 192], F32)
    gvec = sb("gvec_sb", [128, 128], F32)
    modc = [sb(f"modc{c}", [128, 192], F32) for c in range(2)]
    A1 = [sb(f"A1_{c}", [128, 32], F32) for c in range(2)]
    A2 = [sb(f"A2_{c}", [128, 32], F32) for c in range(2)]
    Gf = sb("Gf", [128, 4, 32], F32)
    GT = sb("GT", [32, 4, 128], F32)
    w_sconvT = sb("w_sconvT_sb", [128, 48], F32)
    w_ffn_convT = sb("w_ffn_convT_sb", [128, 3 * FC], F32)
    sinkb = sb("sinkb_sb", [128, 32], F32)
    negsink = sb("negsink", [128, 32], F32)
    stat = sb("stat", [128, 64], F32)
    stat2 = sb("stat2", [128, 4, 16], F32)
    astat = sb("astat", [128, 64], F32)
    halo4 = sb("halo4", [128, 2, 4], F32)
    kvst = sb("kvst", [128, 4, 256], F32)
    junk = sb("junk", [128, 256], F32)
    mask_b = sb("mask_b", [128, 384], BF16)
    ones_f = sb("ones_f", [1, 128], F32)
    hTh = sb("hTh", [128, 32, 2], BF16)

    pdb = [stack.enter_context(nc.psum_tensor(f"pd{i}", [128, 1024], F32)) for i in range(4)]
    ps = [pdb[i // 2][:, (i % 2) * 512:(i % 2 + 1) * 512] for i in range(8)]
    psd = [Dep(excl=True) for _ in range(8)]
    rr = [0]

    def nextps():
        i = rr[0] % 8
        rr[0] += 1
        return ps[i], psd[i]

    def psbf(p):
        return p[:, :].bitcast(BF16)

    def rview(off, n):
        return R[:, off:off + n]

    hT = rview(0, 24576).rearrange("p (c t) -> p c t", c=32)
    scv = rview(24576, 8192).rearrange("p (c t) -> p c t", c=16)
    mo32 = rview(0, 32768).bitcast(F32).rearrange("p (b n) -> p b n", b=4)
    qa = rview(32768, 16384).rearrange("p (c t) -> p c t", c=32)
    xblk0 = rview(32768, 8192).bitcast(F32)
    xn_v = rview(40960, 4096)
    GrowA = rview(40960, 8192).bitcast(F32)
    OA = 49152
    kT = rview(OA, 6144).rearrange("p (c t) -> p c t", c=8)
    vtok = rview(OA + 6144, 6144).rearrange("p (b n) -> p b n", b=6)
    kcT = rview(OA + 12288, 4096).rearrange("p (c t) -> p c t", c=8)
    vc = rview(OA + 16384, 4096).rearrange("p (b n) -> p b n", b=4)
    merged = rview(OA, 16384).rearrange("p (c t) -> p c t", c=32)
    xblk1 = rview(OA, 8192).bitcast(F32)
    actv = rview(0, 44032).rearrange("p (c t) -> p c t", c=FC)
    OH2 = 44032
    hT2 = rview(OH2, 16448).rearrange("p (c t) -> p c t", c=32)
    f_sb = rview(OH2, 16384).rearrange("p (b n) -> p b n", b=4)
    bx0 = rview(0, 8192).bitcast(F32)
    bxn = rview(8192, 4096)
    bx1 = rview(8192, 8192).bitcast(F32)
    btmp = rview(16384, 8192).bitcast(F32)
    GrowB = rview(24576, 8192).bitcast(F32)

    wsem = [P.dma_sem("w"), P.dma_sem("w")]
    wdep = [Dep(), Dep()]
    wn = [0]


    wscr_idx = {}
    wscr_dep = {}
    pend_store = [None]
    ssem = [P.dma_sem("ws"), P.dma_sem("ws")]
    pcsem = P.dma_sem("pc")
    preconv = {"list": [], "pos": 0, "active": False, "cnt": 0, "deps": []}
    for jj_ in range(FC // 2):
        preconv["list"].append((("gt", jj_), wv_up[:, :, jj_ * 256:(jj_ + 1) * 256]))
        preconv["list"].append((("val", jj_), wv_up[:, :, DFF + jj_ * 256:DFF + (jj_ + 1) * 256]))
    for nt_ in range(16):
        for kp_, (k0_, k1_) in enumerate([(0, 32), (32, 64), (64, FC)]):
            preconv["list"].append((("dn", nt_, kp_), wv_down[:, k0_:k1_, nt_ * 256:(nt_ + 1) * 256]))

    def preconv_step():
        if not preconv["active"] or preconv["pos"] >= len(preconv["list"]):
            return
        preconv["cnt"] += 1
        if preconv["cnt"] % PRECONV_EVERY:
            return
        key, src = preconv["list"][preconv["pos"]]
        preconv["pos"] += 1
        nk, ncol = src.shape[1], src.shape[2]
        idx = len(wscr_idx)
        assert idx < NSCR
        wscr_idx[key] = idx
        sd = Dep()
        wscr_dep[key] = sd
        preconv["deps"].append(sd)
        scr = wscr_t[idx // 96][idx % 96, :, 0:nk * ncol].rearrange("p (k n) -> p k n", k=nk)
        P.op("pool", lambda g, scr=scr, src=src: g.dma_start(out=scr, in_=src), dsem=pcsem)

    def wload(src, key=None):
        i = wn[0] % 2
        wn[0] += 1
        nk, ncol = src.shape[1], src.shape[2]
        dst = RW[i][:, 0:nk, 0:ncol]
        if key is not None and key in wscr_idx:
            idx = wscr_idx[key]
            scr = wscr_t[idx // 96][idx % 96, :, 0:nk * ncol].rearrange("p (k n) -> p k n", k=nk)
            P.op("pool", lambda g, dst=dst, scr=scr: g.dma_start(out=dst, in_=scr),
                 reads=[wscr_dep[key]], writes=[wdep[i]], dsem=wsem[i])
            if pend_store[0] is not None:
                pend_store[0]()
                pend_store[0] = None
            preconv_step()
            return RW[i], wdep[i]
        P.op("pool", lambda g, dst=dst, src=src: g.dma_start(out=dst, in_=src),
             writes=[wdep[i]], dsem=wsem[i])
        if pend_store[0] is not None:
            pend_store[0]()
            pend_store[0] = None
        if key is not None:
            idx = len(wscr_idx)
            assert idx < NSCR
            wscr_idx[key] = idx
            sd = Dep()
            wscr_dep[key] = sd
            scr = wscr_t[idx // 96][idx % 96, :, 0:nk * ncol].rearrange("p (k n) -> p k n", k=nk)

            def do_store(i=i, dst=dst, scr=scr, sd=sd):
                P.op("pool", lambda g: g.dma_start(out=scr, in_=dst), reads=[wdep[i]], writes=[sd], dsem=ssem[i])
            pend_store[0] = do_store
        return RW[i], wdep[i]

    csem = P.dma_sem("c")

    def cload(dst, src, dep):
        P.op("sp", lambda e: e.dma_start(out=dst, in_=src), writes=[dep], dsem=csem)

    def act_copy(out, in_, reads, writes):
        return P.op("act", lambda e: e.copy(out, in_), reads=reads, writes=writes)

    def dve_copy(out, in_, reads, writes):
        return P.op("dve", lambda e: e.tensor_copy(out, in_), reads=reads, writes=writes)

    evac_rr = [0]

    def any_copy(out, in_, reads, writes):
        evac_rr[0] += 1
        if evac_rr[0] % 2:
            return act_copy(out, in_, reads, writes)
        return dve_copy(out, in_, reads, writes)

    def pe_group(out, pairs, reads, writes):
        def fn(pe, out=out, pairs=pairs):
            n = len(pairs)
            for i, (l, r) in enumerate(pairs):
                ins = pe.matmul(out, l, r, start=(i == 0), stop=(i == n - 1))
            return ins
        return P.op("pe", fn, reads=reads, writes=writes)

    cd = {n: Dep() for n in ["ident_f", "perm_f", "mask", "condT", "b_modT", "gvec", "w_sconvT",
                             "w_ffn_convT", "sinkb", "ident_b", "perm_b", "scT", "negsink"]}
    cload(ident_f[:, :], ident_d, cd["ident_f"])
    cload(perm_f[:, :], perm_d, cd["perm_f"])
    cload(mask_sb[:, :], mask_d, cd["mask"])
    cload(condT[:, :], condT_d, cd["condT"])
    cload(b_modT[:, :], b_modT_d, cd["b_modT"])
    cload(gvec[:, :], gvec_d, cd["gvec"])
    cload(w_sconvT[:, :], w_sconvT_d, cd["w_sconvT"])
    cload(w_ffn_convT[:, :], w_ffn_convT_d, cd["w_ffn_convT"])
    cload(sinkb[:, :], sinkb_d, cd["sinkb"])
    final_c = (csem[0], csem[1])
    for n in ["ident_f", "perm_f", "mask", "condT", "b_modT", "gvec", "w_sconvT", "w_ffn_convT", "sinkb"]:
        cd[n].w = final_c

    dve_copy(ident_b[:, :], ident_f[:, :], [cd["ident_f"]], [cd["ident_b"]])
    cd["mask_b"] = Dep()
    cd["ones_f"] = Dep()
    dve_copy(mask_b[:, :], mask_sb[:, :], [cd["mask"]], [cd["mask_b"]])
    P.op("dve", lambda e: e.memset(ones_f[:, :], 1.0), writes=[cd["ones_f"]])
    dve_copy(perm_b[:, :], perm_f[:, :], [cd["perm_f"]], [cd["perm_b"]])
    P.op("dve", lambda e: e.tensor_scalar(negsink[:, :], sinkb[:, :], -1.0, None, ALU.mult),
         reads=[cd["sinkb"]], writes=[cd["negsink"]])
    P.op("act", lambda e: e.activation(scT[:, :], condT[:, :], AF.Silu),
         reads=[cd["condT"]], writes=[cd["scT"]])

    scT3 = scT[:, :].rearrange("p (k c) -> p k c", c=2)
    mps, mpd = nextps()
    mps3 = mps[:, 0:384].rearrange("p (f c) -> p f c", c=2)
    mod_last = None
    for t in range(96):
        wt, wd = wload(wv_mod[:, :, t * 256:(t + 1) * 256])
        for hh in range(2):
            fc = 2 * t + hh
            pairs = [(wt[:, kc, hh * 128:(hh + 1) * 128], scT3[:, kc, :]) for kc in range(KC)]
            pe_group(mps3[:, fc, :], pairs, reads=[wd, cd["scT"]], writes=[mpd])
    modd = [Dep(), Dep()]
    for c in range(2):
        P.op("dve", lambda e, c=c: e.tensor_tensor(modc[c][:, :], mps3[:, :, c], b_modT[:, :], ALU.add),
             reads=[mpd, cd["b_modT"]], writes=[modd[c]])
    gv = gvec[:, :].rearrange("p (g k) -> p g k", g=4)
    moddep = Dep()
    for c in range(2):
        P.op("dve", lambda e, c=c: e.scalar_tensor_tensor(A1[c][:, :], modc[c][:, 32:64], 1.0, gv[:, 0, :],
                                                           ALU.add, ALU.mult),
             reads=[modd[c], cd["gvec"]], writes=[moddep])
        P.op("dve", lambda e, c=c: e.scalar_tensor_tensor(A2[c][:, :], modc[c][:, 128:160], 1.0, gv[:, 2, :],
                                                           ALU.add, ALU.mult),
             reads=[modd[c], cd["gvec"]], writes=[moddep])
        P.op("dve", lambda e, c=c: e.tensor_tensor(Gf[:, c, :], modc[c][:, 64:96], gv[:, 1, :], ALU.mult),
             reads=[modd[c], cd["gvec"]], writes=[moddep])
        P.op("dve", lambda e, c=c: e.tensor_tensor(Gf[:, 2 + c, :], modc[c][:, 160:192], gv[:, 3, :], ALU.mult),
             reads=[modd[c], cd["gvec"]], writes=[moddep])
    B1 = [modc[c][:, 0:32] for c in range(2)]
    B2 = [modc[c][:, 96:128] for c in range(2)]
    growd = Dep()
    gtd = Dep()
    gsem = P.dma_sem("g")
    for r in range(4):
        tp, tpd = nextps()
        P.op("pe", lambda pe, r=r, tp=tp: pe.transpose(tp[0:32, 0:128], Gf[:, r, :], ident_f[:, :]),
             reads=[moddep, cd["ident_f"]], writes=[tpd])
        dve_copy(GT[:, r, :], tp[0:32, 0:128], [tpd], [gtd])
        P.op("sp", lambda e, r=r: e.dma_start(out=grow[r, :].rearrange("(k p) -> k p", p=128), in_=GT[:, r, :]),
             reads=[gtd], writes=[growd], dsem=gsem)

    xsem = [P.dma_sem("x"), P.dma_sem("x")]
    hsem = P.dma_sem("h")
    hsemq = [hsem, P.dma_sem("h")]
    osem = [P.dma_sem("o"), P.dma_sem("o")]
    kvsem = [P.dma_sem("kv") for _ in range(4)]
    kvd = [Dep() for _ in range(4)]
    kvn = [0]
    x1p_d = [Dep() for _ in range(8)]
    x1s_d = [Dep() for _ in range(16)]
    out_events = []

    hsd = [Dep(), Dep()]

    def h_steps(xsrc, blocks, xb, xbd, xnv, xnd, Am, Bm, hdst, hd, extra_reads, split=False):
        if not isinstance(xb, list):
            xb, xbd, xnv, xnd = [xb], [xbd], [xnv], [xnd]
        nbuf = len(xb)
        triples = []
        for bi, blk in enumerate(blocks):
            q = bi % nbuf

            def fL(blk=blk, q=q):
                for (r0, cnt, p0) in blk["rows"]:
                    P.op("sp", lambda e, r0=r0, cnt=cnt, p0=p0: e.dma_start(out=xb[q][p0:p0 + cnt, :], in_=xsrc[r0:r0 + cnt, :]),
                         reads=blk.get("srcdeps", []), writes=[xbd[q]], dsem=hsemq[q])

            def fN(blk=blk, q=q):
                n = blk["n"]
                sd = hsd[q]
                s0 = 4 * q
                P.op("act", lambda e: e.activation(xnv[q][0:n, :], xb[q][0:n, :], AF.Square, accum_out=stat[0:n, s0:s0 + 1]),
                     reads=[xbd[q]], writes=[xnd[q], sd])
                P.op("act", lambda e: e.activation(stat[0:n, s0 + 1:s0 + 2], stat[0:n, s0:s0 + 1], AF.Sqrt, bias=EPS, scale=1.0 / D),
                     reads=[sd], writes=[sd])
                P.op("dve", lambda e: e.reciprocal(stat[0:n, s0 + 2:s0 + 3], stat[0:n, s0 + 1:s0 + 2]), reads=[sd], writes=[sd])
                P.op("dve", lambda e: e.tensor_scalar(xnv[q][0:n, :], xb[q][0:n, :], stat[0:n, s0 + 2:s0 + 3], None, ALU.mult),
                     reads=[xbd[q], sd], writes=[xnd[q]])

            def fT(blk=blk, q=q):
                n = blk["n"]
                for cg in range(8):
                    tp, tpd = nextps()
                    tpb = psbf(tp)[:, 0:512].rearrange("p (i t) -> p i t", i=4)

                    def tfn(pe, cg=cg, tpb=tpb, n=n):
                        for i in range(4):
                            c = cg * 4 + i
                            ins = pe.transpose(tpb[:, i, 0:n], xnv[q][0:n, c * 128:(c + 1) * 128], ident_b[0:n, 0:n])
                        return ins
                    P.op("pe", tfn, reads=[xnd[q], cd["ident_b"]], writes=[tpd])
                    for i in range(4):
                        c = cg * 4 + i
                        for (pidx, cnt, col) in blk["dst"]:
                            o = hdst[:, c, col:col + cnt]
                            src = tpb[:, i, pidx:pidx + cnt]
                            if c % 2 == 0:
                                P.op("act", lambda e, o=o, src=src, c=c: e.activation(o, src, AF.Identity,
                                                                                        bias=Bm[:, c:c + 1], scale=Am[:, c:c + 1]),
                                     reads=[tpd, moddep] + extra_reads, writes=[hd[c]])
                            else:
                                P.op("dve", lambda e, o=o, src=src, c=c: e.tensor_scalar(o, src, Am[:, c:c + 1], Bm[:, c:c + 1],
                                                                                          ALU.mult, ALU.add),
                                     reads=[tpd, moddep] + extra_reads, writes=[hd[c]])
            triples.append((fL, fN, fT))
        if split:
            return triples
        steps = []
        for (fL, fN, fT) in triples:
            steps += [(lambda fL=fL, fN=fN: (fL(), fN())), fT]
        return steps

    reg_users = {"H": [], "S": [], "Q": [], "O": []}

    def claim(regs, deps):
        for r_ in regs:
            alias(deps, reg_users[r_])
        for r_ in regs:
            reg_users[r_] = reg_users[r_] + list(deps)

    fscr_v = fscr.rearrange("(b p) n -> p b n", p=128)
    fscrd = Dep()
    s2dB = [Dep() for _ in range(4)]
    esd = Dep()
    jdB = Dep()
    ropd_g = Dep()
    fssem = P.dma_sem("fs")
    flsem = P.dma_sem("fl")

    def epi_steps(EFb, EXb, EGb, EFd_, EXd_, EGd_, r, c_lo, xsrc, src_deps, dst, dst_deps, is_out):
        gb0 = c_lo // 128
        P.op("sp", lambda e: e.dma_start(out=EGb, in_=grow[r:r + 1, :].partition_broadcast(128)),
             reads=[growd], writes=[EGd_], dsem=gsem)

        def elf(b):
            P.op("sp", lambda e: e.dma_start(out=EFb, in_=fscr[b * 128:(b + 1) * 128, :]),
                 reads=[fscrd], writes=[EFd_], dsem=flsem)

        def elx(b):
            k = b % 2
            P.op("sp", lambda e: e.dma_start(out=EXb[k], in_=xsrc[c_lo + b * 128:c_lo + (b + 1) * 128, :]),
                 reads=[src_deps[gb0 + b]] if src_deps is not None else [], writes=[EXd_[k]], dsem=xsem[k])

        def ec(b):
            k = b % 2
            c0 = 8 + b * 4
            P.op("dve", lambda e: e.reduce_sum(stat[:, c0:c0 + 1], stat2[:, b, :], axis=AX.X),
                 reads=[s2dB[b]], writes=[esd])
            P.op("act", lambda e: e.activation(stat[:, c0 + 1:c0 + 2], stat[:, c0:c0 + 1], AF.Sqrt, bias=EPS, scale=1.0 / D),
                 reads=[esd], writes=[esd])
            P.op("dve", lambda e: e.reciprocal(stat[:, c0 + 2:c0 + 3], stat[:, c0 + 1:c0 + 2]), reads=[esd], writes=[esd])
            P.op("dve", lambda e: e.tensor_tensor(EFb, EFb, EGb, ALU.mult), reads=[EGd_], writes=[EFd_])
            P.op("dve", lambda e: e.scalar_tensor_tensor(EXb[k], EFb, stat[:, c0 + 2:c0 + 3], EXb[k], ALU.mult, ALU.add),
                 reads=[EFd_, esd], writes=[EXd_[k]])
            ev = P.op("sp", lambda e: e.dma_start(out=dst[c_lo + b * 128:c_lo + (b + 1) * 128, :], in_=EXb[k]),
                      reads=[EXd_[k]], writes=[dst_deps[gb0 + b]] if not is_out else [], dsem=osem[k])
            if is_out:
                out_events.append(ev)
        elf(0)
        elx(0)
        elx(1)
        return [[lambda: ec(0), lambda: elf(1)],
                [lambda: ec(1), lambda: elf(2), lambda: elx(2)],
                [lambda: ec(2), lambda: elf(3), lambda: elx(3)],
                [lambda: ec(3)]]

    rk_prev = [[]]

    def new_work(n):
        ds = [Dep() for _ in range(n)]
        alias(ds, rk_prev[0])
        rk_prev[0] = ds
        return ds

    hxb = [rview(24576, 8192).bitcast(F32), rview(32768, 8192).bitcast(F32)]
    hxn = [rview(40960, 4096), rview(45056, 4096)]
    FSTAf = rview(65536, 4096).bitcast(F32)
    fstA = [FSTAf[:, k * 1024:(k + 1) * 1024].rearrange("p (b n) -> p b n", b=4) for k in range(2)]
    EFa = rview(32768, 8192).bitcast(F32)
    EXa = [rview(40960, 8192).bitcast(F32), rview(49152, 8192).bitcast(F32)]
    EGa = rview(57344, 8192).bitcast(F32)

    def a_geom(kind, ti):
        c_lo = 512 * ti
        if kind == "p":
            win_lo, win_hi = c_lo, c_lo + 512
        else:
            win_lo, win_hi = max(0, c_lo - 128), min(2048, c_lo + 640)
        return c_lo, win_lo, win_hi

    def a_make(kind, ti):
        cond = 0 if kind == "p" else 1
        xsrc = xp if kind == "p" else xs
        c_lo, win_lo, win_hi = a_geom(kind, ti)
        WB = (win_hi - win_lo) // 128
        hd = [Dep() for _ in range(32)]
        claim(["H"], hd)
        xbd = [Dep(), Dep()]
        xnd = [Dep(), Dep()]
        claim(["S"], [xbd[0]])
        claim(["Q"], [xbd[1]] + xnd)
        blocks = [dict(n=128, rows=[(win_lo + b * 128, 128, 0)], dst=[(0, 128, b * 128)]) for b in range(WB)]
        lnt = h_steps(xsrc, blocks, hxb, xbd, hxn, xnd, A1[cond], B1[cond], hT, hd, [], split=True)
        return dict(kind=kind, ti=ti, hd=hd, lnt=lnt)

    def pass_a_tile(ctx, epi, nxt):
        kind, ti = ctx["kind"], ctx["ti"]
        cond = 0 if kind == "p" else 1
        xsrc = xp if kind == "p" else xs
        c_lo, win_lo, win_hi = a_geom(kind, ti)
        if kind == "p":
            nseg, L = 2, 256
        else:
            nseg, L = 1, 512
        W = win_hi - win_lo
        WB = W // 128
        cofs = c_lo - win_lo
        hd = ctx["hd"]
        scd = [Dep() for _ in range(16)]
        claim(["S"], scd)

        cw = new_work(8)
        LP = L + 2
        mbuf = [RK[:, 0:516], RK[:, 516:1032]]
        ccs = [RK[:, 1032:1544], RK[:, 1544:2056]]
        tcv = [RK[:, 2056:2568], RK[:, 2568:3080]]
        hrows = []
        if kind == "s":
            if cofs - 1 >= 0:
                hrows.append((cofs - 1, 0))
            if cofs + 512 < W:
                hrows.append((cofs + 512, 513))
        for k in range(2):
            P.op("dve", lambda e, k=k: e.memset(mbuf[k], 0.0), writes=[cw[k]])
        hdep = Dep()
        nhal = len(hrows)

        hThd = Dep()
        alias([hThd], tile_state["hTh"])
        tile_state["hTh"] = [hThd]
        if nhal:
            c0 = hrows[0][0]
            hsrc = hT[:, :, c0:c0 + 514:513] if nhal == 2 else hT[:, :, c0:c0 + 1]
            P.op("dve", lambda e: e.tensor_copy(hTh[:, :, 0:nhal], hsrc), reads=hd, writes=[hThd])
        cn = 0
        for jj in range(8):
            wcc, wccd = wload(wv_in[:, :, I_CC + jj * 256:I_CC + (jj + 1) * 256], ("cc", jj))
            cc_ps = []
            for hh in range(2):
                pt, ptd = nextps()
                pairs = [(wcc[:, kc, hh * 128:(hh + 1) * 128], hT[:, kc, cofs:cofs + 512]) for kc in range(KC)]
                pe_group(pt[:, :], pairs, reads=[wccd] + hd, writes=[ptd])
                hp = None
                if hrows:
                    hp, hpd = nextps()
                    pairs = [(wcc[:, kc, hh * 128:(hh + 1) * 128], hTh[:, kc, 0:nhal]) for kc in range(KC)]
                    pe_group(hp[:, 0:nhal], pairs, reads=[wccd, hThd], writes=[hpd])
                    cc_ps.append((pt, ptd, hp, hpd))
                else:
                    cc_ps.append((pt, ptd, None, None))
            wch, wchd = wload(wv_in[:, :, I_CH + jj * 256:I_CH + (jj + 1) * 256], ("ch", jj))
            mks = []
            for hh in range(2):
                k = cn % 2
                cn += 1
                pt, ptd, hp, hpd = cc_ps[hh]
                act_copy(ccs[k], pt[:, :], [ptd], [cw[2 + k]])
                p2, p2d = nextps()
                pairs = [(wch[:, kc, hh * 128:(hh + 1) * 128], hT[:, kc, cofs:cofs + 512]) for kc in range(KC)]
                pe_group(p2[:, :], pairs, reads=[wchd] + hd, writes=[p2d])
                mb3 = mbuf[k][:, 0:nseg * LP].rearrange("p (s l) -> p s l", s=nseg)
                P.op("dve", lambda e, mb3=mb3, k=k, p2=p2: e.tensor_tensor(
                    mb3[:, :, 1:L + 1], ccs[k].rearrange("p (s l) -> p s l", s=nseg),
                    p2[:, :].rearrange("p (s l) -> p s l", s=nseg), ALU.mult),
                    reads=[cw[2 + k], p2d], writes=[cw[k]])
                if hrows:
                    pairs = [(wch[:, kc, hh * 128:(hh + 1) * 128], hTh[:, kc, 0:nhal]) for kc in range(KC)]
                    pe_group(hp[:, 2:2 + nhal], pairs, reads=[wchd, hThd], writes=[hpd])
                    act_copy(halo4[:, k, :], hp[:, 0:4], [hpd], [hdep])
                    mc0 = hrows[0][1]
                    mdst = mbuf[k][:, mc0:mc0 + 514:513] if nhal == 2 else mbuf[k][:, mc0:mc0 + 1]
                    P.op("dve", lambda e, k=k, mdst=mdst: e.tensor_tensor(
                        mdst, halo4[:, k, 0:nhal], halo4[:, k, 2:2 + nhal], ALU.mult),
                        reads=[hdep], writes=[cw[k]])
                mks.append(k)
            wcb, wcbd = wload(wv_in[:, :, I_CB + jj * 256:I_CB + (jj + 1) * 256], ("cb", jj))
            for hh in range(2):
                j = 2 * jj + hh
                k = mks[hh]
                mb3 = mbuf[k][:, 0:nseg * LP].rearrange("p (s l) -> p s l", s=nseg)
                t3 = tcv[k].rearrange("p (s l) -> p s l", s=nseg)
                wc_ = w_sconvT[:, :].rearrange("p (k c) -> p k c", k=3)
                P.op("dve", lambda e, mb3=mb3, t3=t3, j=j: e.tensor_scalar(t3, mb3[:, :, 1:L + 1], wc_[:, 1, j:j + 1], None, ALU.mult),
                     reads=[cw[k], cd["w_sconvT"]], writes=[cw[4 + k]])
                P.op("dve", lambda e, mb3=mb3, t3=t3, j=j: e.scalar_tensor_tensor(t3, mb3[:, :, 0:L], wc_[:, 0, j:j + 1], t3, ALU.mult, ALU.add),
                     reads=[cw[k]], writes=[cw[4 + k]])
                P.op("dve", lambda e, mb3=mb3, t3=t3, j=j: e.scalar_tensor_tensor(t3, mb3[:, :, 2:L + 2], wc_[:, 2, j:j + 1], t3, ALU.mult, ALU.add),
                     reads=[cw[k]], writes=[cw[4 + k]])
                pt, ptd = nextps()
                pairs = [(wcb[:, kc, hh * 128:(hh + 1) * 128], hT[:, kc, cofs:cofs + 512]) for kc in range(KC)]
                pe_group(pt[:, :], pairs, reads=[wcbd] + hd, writes=[ptd])
                P.op("dve", lambda e, pt=pt, k=k, j=j: e.tensor_tensor(scv[:, j, :], pt[:, :], tcv[k], ALU.mult),
                     reads=[ptd, cw[4 + k]], writes=[scd[j]])
            if jj < len(epi):
                for s in epi[jj]:
                    s()

        qad = [[Dep() for _ in range(4)] for _ in range(32)]
        kTd = [Dep() for _ in range(8)]
        vd = [Dep() for _ in range(6)]
        kcTd = [Dep() for _ in range(8)]
        vcd = Dep()
        mrgd = [Dep() for _ in range(32)]
        claim(["Q"], [d for row in qad for d in row])
        claim(["O"], kTd + vd + kcTd + [vcd])

        ropd = None
        if kind == "s":
            ropd = ropd_g
            cload(ropeC[:, 0:W], ropeC_d[:, win_lo:win_hi], ropd)
            cload(ropeS[:, 0:W], ropeS_d[:, win_lo:win_hi], ropd)

        wk = new_work(9)
        kraw = [RK[:, 0:256].bitcast(BF16), RK[:, 256:512].bitcast(BF16)]
        t1b = [RK[:, 512:1024], RK[:, 1024:1536]]
        t2b = [RK[:, 1536:2048], RK[:, 2048:2560]]
        ropn = [0]

        def evac_fm(psrc, psdep, n, dst, dstdep, wcol0):
            if kind == "p":
                any_copy(dst, psrc, [psdep], [dstdep])
                return
            k = ropn[0] % 2
            ropn[0] += 1
            kr, t1, t2 = kraw[k][:, 0:n], t1b[k][:, 0:n], t2b[k][:, 0:n]
            act_copy(kr, psrc, [psdep], [wk[k]])
            p2, p2d = nextps()
            pe_group(p2[:, 0:n], [(perm_b[:, :], kr)], reads=[wk[k], cd["perm_b"]], writes=[p2d])
            P.op("dve", lambda e: e.tensor_tensor(t1, psrc, ropeC[:, wcol0:wcol0 + n], ALU.mult),
                 reads=[psdep, ropd], writes=[wk[2 + k]])
            P.op("dve", lambda e: e.tensor_tensor(t2, p2[:, 0:n], ropeS[:, wcol0:wcol0 + n], ALU.mult),
                 reads=[p2d, ropd], writes=[wk[4 + k]])
            P.op("dve", lambda e: e.tensor_tensor(dst, t1, t2, ALU.add),
                 reads=[wk[2 + k], wk[4 + k]], writes=[dstdep])

        if W <= 512:
            segs = [(0, W)]
        else:
            segs = [(0, W // 2), (W // 2, W)]

        def kv_out(pt, ptd, dram, row0, col0):
            i = kvn[0] % 4
            kvn[0] += 1
            any_copy(kvst[:, i, :], pt[:, 0:256], [ptd], [kvd[i]])
            ev = P.op("sp", lambda e: e.dma_start(out=dram[row0:row0 + 128, col0:col0 + 256], in_=kvst[:, i, :]),
                      reads=[kvd[i]], dsem=kvsem[i])
            out_events.append(ev)

        for jj in range(4):
            wt, wd = wload(wv_in[:, :, I_K + jj * 256:I_K + (jj + 1) * 256], ("k", jj))
            for hh in range(2):
                g = 2 * jj + hh
                for (s0, s1) in segs:
                    pt, ptd = nextps()
                    pairs = [(wt[:, kc, hh * 128:(hh + 1) * 128], hT[:, kc, s0:s1]) for kc in range(KC)]
                    pe_group(pt[:, 0:s1 - s0], pairs, reads=[wd] + hd, writes=[ptd])
                    evac_fm(pt[:, 0:s1 - s0], ptd, s1 - s0, kT[:, g, s0:s1], kTd[g], s0)
            if kind == "p":
                for b in range(4):
                    pt, ptd = nextps()
                    pairs = [(hT[:, kc, b * 128:(b + 1) * 128], wt[:, kc, :]) for kc in range(KC)]
                    pe_group(pt[:, 0:256], pairs, reads=[wd] + hd, writes=[ptd])
                    kv_out(pt, ptd, nk_o, c_lo + b * 128, jj * 256)
        for jj in range(4):
            wt, wd = wload(wv_in[:, :, I_V + jj * 256:I_V + (jj + 1) * 256], ("v", jj))
            for b in range(WB):
                pt, ptd = nextps()
                pairs = [(hT[:, kc, b * 128:(b + 1) * 128], wt[:, kc, :]) for kc in range(KC)]
                pe_group(pt[:, 0:256], pairs, reads=[wd] + hd, writes=[ptd])
                any_copy(vtok[:, b, jj * 256:(jj + 1) * 256], pt[:, 0:256], [ptd], [vd[b]])
                if kind == "p":
                    kv_out(pt, ptd, nv_o, c_lo + b * 128, jj * 256)

        if kind == "s":
            ckt = RK[:, 2560:4096].bitcast(BF16)
            for kb in range(4):
                sl = kb % 3
                slot = ckt[:, sl * 1024:(sl + 1) * 1024]
                sd_ = wk[6 + sl]
                P.op("pool", lambda g_, slot=slot, kb=kb: g_.dma_start(out=slot, in_=ck[kb * 128:(kb + 1) * 128, :]),
                     writes=[sd_], dsem=cksem[sl])
                for g2 in range(2):
                    tp, tpd = nextps()
                    tpb = psbf(tp)[:, 0:512].rearrange("p (i t) -> p i t", i=4)

                    def tfn(pe, tpb=tpb, slot=slot, g2=g2):
                        for i in range(4):
                            g = g2 * 4 + i
                            ins = pe.transpose(tpb[:, i, :], slot[:, g * 128:(g + 1) * 128], ident_b[:, :])
                        return ins
                    P.op("pe", tfn, reads=[sd_, cd["ident_b"]], writes=[tpd])
                    for i in range(4):
                        g = g2 * 4 + i
                        any_copy(kcT[:, g, kb * 128:(kb + 1) * 128], tpb[:, i, :], [tpd], [kcTd[g]])
            P.op("pool", lambda g_: g_.dma_start(out=vc, in_=cv.rearrange("(b p) n -> p b n", p=128)),
                 writes=[vcd], dsem=cksem[3])

        for jj in range(16):
            wt, wd = wload(wv_in[:, :, jj * 256:(jj + 1) * 256], ("q", jj))
            for hh in range(2):
                h = 2 * jj + hh
                pt, ptd = nextps()
                pairs = [(wt[:, kc, hh * 128:(hh + 1) * 128], hT[:, kc, cofs:cofs + 512]) for kc in range(KC)]
                pe_group(pt[:, :], pairs, reads=[wd] + hd, writes=[ptd])
                if kind == "p":
                    evac_rr[0] += 1
                    if evac_rr[0] % 2:
                        P.op("act", lambda e, h=h, pt=pt: e.mul(qa[:, h, :], pt[:, :], SCALE), reads=[ptd], writes=qad[h])
                    else:
                        P.op("dve", lambda e, h=h, pt=pt: e.tensor_scalar(qa[:, h, :], pt[:, :], SCALE, None, ALU.mult),
                             reads=[ptd], writes=qad[h])
                else:
                    k = ropn[0] % 2
                    ropn[0] += 1
                    kr, t1, t2 = kraw[k], t1b[k], t2b[k]
                    act_copy(kr, pt[:, :], [ptd], [wk[k]])
                    p2, p2d = nextps()
                    pe_group(p2[:, :], [(perm_b[:, :], kr)], reads=[wk[k], cd["perm_b"]], writes=[p2d])
                    P.op("dve", lambda e, t1=t1, pt=pt: e.scalar_tensor_tensor(t1, pt[:, :], SCALE, ropeC[:, cofs:cofs + 512],
                                                                                ALU.mult, ALU.mult),
                         reads=[ptd, ropd], writes=[wk[2 + k]])
                    P.op("dve", lambda e, t2=t2, p2=p2: e.scalar_tensor_tensor(t2, p2[:, :], SCALE, ropeS[:, cofs:cofs + 512],
                                                                                ALU.mult, ALU.mult),
                         reads=[p2d, ropd], writes=[wk[4 + k]])
                    P.op("dve", lambda e, t1=t1, t2=t2, h=h: e.tensor_tensor(qa[:, h, :], t1, t2, ALU.add),
                         reads=[wk[2 + k], wk[4 + k]], writes=qad[h])

        aw = new_work(4)
        p_sb = [RK[:, 0:512].bitcast(BF16), RK[:, 512:1024].bitcast(BF16)]
        pT_sb = [RK[:, 1024:1536].bitcast(BF16), RK[:, 1536:2048].bitcast(BF16)]
        items = []
        if kind == "p":
            for s in range(2):
                for qb in range(2):
                    for h in range(NH):
                        g = h // 4
                        q0 = s * 256 + qb * 128
                        items.append(dict(h=h, g=g, qblk=q0 // 128, q0=q0, k0=s * 256, k1=(s + 1) * 256, mask=None,
                                          vblocks=[(vtok, 2 * s + kb, vd[2 * s + kb]) for kb in range(2)], ctx=False))
        else:
            for qb in range(4):
                gb = c_lo // 128 + qb
                wb = gb - win_lo // 128
                has_l, has_r = gb - 1 >= 0, gb + 1 < 16
                b0 = wb - 1 if has_l else wb
                b1 = wb + 1 if has_r else wb
                m0 = 0 if has_l else 128
                m1 = 384 if has_r else 256
                for h in range(NH):
                    g = h // 4
                    items.append(dict(h=h, g=g, qblk=qb, q0=qb * 128, k0=b0 * 128, k1=(b1 + 1) * 128, mask=(m0, m1),
                                      vblocks=[(vtok, b, vd[b]) for b in range(b0, b1 + 1)], ctx=True))
        n_it = len(items)
        st_d = [Dep() for _ in range(n_it)]

        def step1a(i):
            it = items[i]
            k = i % 2
            h, g, q0 = it["h"], it["g"], it["q0"]
            nlat = it["k1"] - it["k0"]
            nk = nlat + (512 if it["ctx"] else 0)
            it["nk"] = nk
            it["nlat"] = nlat
            pd_ = pdb[k]
            d0, d1 = psd[2 * k], psd[2 * k + 1]
            cs = 512 - nlat - 1
            rd = [qad[h][it["qblk"]], kTd[g], cd["ones_f"], cd["sinkb"]]
            wr = [d0]
            if it["mask"] is not None:
                rd += [cd["mask_b"], cd["ident_b"]]
            if it["ctx"]:
                rd.append(kcTd[g])
                wr.append(d1)

            def fn(pe):
                has_m = it["mask"] is not None
                pe.matmul(pd_[:, cs:cs + 1], ones_f[0:1, :], sinkb[0:1, h:h + 1], start=True, stop=False,
                          skip_group_check=True)
                ins = pe.matmul(pd_[:, 512 - nlat:512], qa[:, h, q0:q0 + 128], kT[:, g, it["k0"]:it["k1"]],
                                start=False, stop=not has_m, skip_group_check=True)
                if has_m:
                    m0, m1 = it["mask"]
                    ins = pe.matmul(pd_[:, 512 - nlat:512], ident_b[:, :], mask_b[:, m0:m1], start=False, stop=True,
                                    skip_group_check=True)
                if it["ctx"]:
                    ins = pe.matmul(pd_[:, 512:1024], qa[:, h, q0:q0 + 128], kcT[:, g, :], start=True, stop=True)
                return ins
            P.op("pe", fn, reads=rd, writes=wr)
            sd = st_d[i]
            c0 = (i % 8) * 8
            stv = astat[:, c0:c0 + 8]
            alias([sd], [st_d[i - 8]] if i >= 8 else [])
            sall = pd_[:, cs:cs + 1 + nk]
            P.op("dve", lambda e: e.reduce_max(stv[:, 0:1], sall, axis=AX.X, negate=True), reads=wr, writes=[sd])
            it["s1"] = (wr, sd, stv, sall)

        def step1b(i):
            it = items[i]
            k = i % 2
            nk = it["nk"]
            wr, sd, stv, sall = it["s1"]
            pdep = aw[k]
            P.op("act", lambda e: e.activation(p_sb[k][:, 1:2 + nk], sall, AF.Exp, bias=stv[:, 0:1], scale=1.0,
                                               accum_out=stv[:, 1:2]),
                 reads=wr + [sd], writes=[pdep, sd])
            P.op("dve", lambda e: e.reciprocal(stv[:, 2:3], stv[:, 1:2]), reads=[sd], writes=[sd])
            P.op("dve", lambda e: e.tensor_scalar(p_sb[k][:, 2:2 + nk], p_sb[k][:, 2:2 + nk], stv[:, 2:3], None, ALU.mult),
                 reads=[sd], writes=[pdep])

        def step2(i):
            it = items[i]
            k = i % 2
            nk = it["nk"]
            nb = nk // 128
            tp, tpd = ps[4 + k], psd[4 + k]
            tpb = psbf(tp)

            def fn(pe):
                for j in range(nb):
                    ins = pe.transpose(tpb[:, j * 128:(j + 1) * 128], p_sb[k][:, 2 + j * 128:2 + (j + 1) * 128], ident_b[:, :])
                return ins
            P.op("pe", fn, reads=[aw[k], cd["ident_b"]], writes=[tpd])
            act_copy(pT_sb[k][:, 0:nk], tpb[:, 0:nk], [tpd], [aw[2 + k]])

        def step3(i):
            it = items[i]
            k = i % 2
            h, g, q0 = it["h"], it["g"], it["q0"]
            po, pod = ps[6 + k], psd[6 + k]
            pairs = []
            rd = [aw[2 + k]]
            j = 0
            for (vt, b, dep) in it["vblocks"]:
                pairs.append((vt[:, b, g * 128:(g + 1) * 128], pT_sb[k][:, j * 128:(j + 1) * 128]))
                rd.append(dep)
                j += 1
            if it["ctx"]:
                for kb in range(4):
                    pairs.append((vc[:, kb, g * 128:(g + 1) * 128], pT_sb[k][:, j * 128:(j + 1) * 128]))
                    j += 1
                rd.append(vcd)
            pe_group(po[:, 0:128], pairs, reads=rd, writes=[pod])
            dve_copy(qa[:, h, q0:q0 + 128], po[:, 0:128], [pod], [qad[h][it["qblk"]]])

        for i in range(n_it + 3):
            if i < n_it:
                step1a(i)
            if 0 <= i - 1 < n_it:
                step1b(i - 1)
            if 0 <= i - 2 < n_it:
                step2(i - 2)
            if 0 <= i - 3 < n_it:
                step3(i - 3)
        qflat = [d for row in qad for d in row]

        claim(["O"], mrgd)
        gw = new_work(8)
        sga = [RK[:, i * 512:(i + 1) * 512] for i in range(4)]
        sgc = [RK[:, (4 + i) * 512:(5 + i) * 512] for i in range(4)]
        for jj in range(16):
            ks = [(2 * jj + hh) % 4 for hh in range(2)]
            wt, wd = wload(wv_in[:, :, I_GA + jj * 256:I_GA + (jj + 1) * 256], ("ga", jj))
            for hh in range(2):
                k = ks[hh]
                pt, ptd = nextps()
                pairs = [(wt[:, kc, hh * 128:(hh + 1) * 128], hT[:, kc, cofs:cofs + 512]) for kc in range(KC)]
                pe_group(pt[:, :], pairs, reads=[wd] + hd, writes=[ptd])
                P.op("act", lambda e, k=k, pt=pt: e.activation(sga[k], pt[:, :], AF.Sigmoid), reads=[ptd], writes=[gw[k]])
            wt, wd = wload(wv_ao[:, :, jj * 256:(jj + 1) * 256], ("ao", jj))
            for hh in range(2):
                k = ks[hh]
                pt, ptd = nextps()
                pairs = [(wt[:, kc, hh * 128:(hh + 1) * 128], qa[:, kc, :]) for kc in range(KC)]
                pe_group(pt[:, :], pairs, reads=[wd] + qflat, writes=[ptd])
                P.op("dve", lambda e, k=k, pt=pt: e.tensor_tensor(sga[k], pt[:, :], sga[k], ALU.mult), reads=[ptd, gw[k]], writes=[gw[k]])
            wt, wd = wload(wv_in[:, :, I_GC + jj * 256:I_GC + (jj + 1) * 256], ("gc", jj))
            for hh in range(2):
                k = ks[hh]
                pt, ptd = nextps()
                pairs = [(wt[:, kc, hh * 128:(hh + 1) * 128], hT[:, kc, cofs:cofs + 512]) for kc in range(KC)]
                pe_group(pt[:, :], pairs, reads=[wd] + hd, writes=[ptd])
                P.op("act", lambda e, k=k, pt=pt: e.activation(sgc[k], pt[:, :], AF.Sigmoid), reads=[ptd], writes=[gw[4 + k]])
            wt, wd = wload(wv_co[:, :, jj * 256:(jj + 1) * 256], ("co", jj))
            for hh in range(2):
                k = ks[hh]
                j = 2 * jj + hh
                pt, ptd = nextps()
                pairs = [(wt[:, kc, hh * 128:(hh + 1) * 128], scv[:, kc, :]) for kc in range(16)]
                pe_group(pt[:, :], pairs, reads=[wd] + scd, writes=[ptd])
                P.op("dve", lambda e, k=k, pt=pt: e.tensor_tensor(sgc[k], pt[:, :], sgc[k], ALU.mult), reads=[ptd, gw[4 + k]], writes=[gw[4 + k]])
                P.op("dve", lambda e, k=k, j=j: e.tensor_tensor(merged[:, j, :], sga[k], sgc[k], ALU.add),
                     reads=[gw[k], gw[4 + k]], writes=[mrgd[j]])

        nctx = a_make(*nxt) if nxt is not None else None
        groups = {}
        if nctx is not None:
            for bi, (fL, fN, fT) in enumerate(nctx["lnt"]):
                groups.setdefault(bi, []).append((1, fL))
                groups.setdefault(bi + 2, []).append((0, fN))
                groups.setdefault(bi + 3, []).append((2, fT))
            assert max(groups) < 16
        fstd_a = [[Dep() for _ in range(4)] for _ in range(2)]
        claim(["O"], [d for row in fstd_a for d in row])
        for nt in range(16):
            wt, wd = wload(wv_mix[:, :, nt * 256:(nt + 1) * 256], ("mix", nt))
            k = nt % 2
            for b in range(4):
                pt, ptd = nextps()
                pairs = [(merged[:, kc, b * 128:(b + 1) * 128], wt[:, kc, :]) for kc in range(KC)]
                pe_group(pt[:, 0:256], pairs, reads=[wd] + mrgd, writes=[ptd])
                any_copy(fstA[k][:, b, :], pt[:, 0:256], [ptd], [fstd_a[k][b]])
                P.op("act", lambda e, pt=pt, b=b, nt=nt: e.activation(junk[:, :], pt[:, 0:256], AF.Square,
                                                                       accum_out=stat2[:, b, nt:nt + 1]),
                     reads=[ptd], writes=[jdB, s2dB[b]])
            P.op("sp", lambda e, k=k, nt=nt: e.dma_start(out=fscr_v[:, :, nt * 256:(nt + 1) * 256], in_=fstA[k]),
                 reads=fstd_a[k], writes=[fscrd], dsem=fssem)
            for (_, s) in sorted(groups.get(nt, []), key=lambda t: t[0]):
                s()
        EFd_, EXd_, EGd_ = Dep(), [Dep(), Dep()], Dep()
        claim(["Q"], [EFd_, EXd_[0]])
        claim(["O"], [EXd_[1], EGd_])
        nepi = epi_steps(EFa, EXa, EGa, EFd_, EXd_, EGd_, cond, c_lo, xsrc, None,
                         x1p if kind == "p" else x1s, x1p_d if kind == "p" else x1s_d, False)
        return nepi, nctx

    def final_epilogue(kind, cond, sub, c_lo, xsrc, fbuf, fdeps, s2d, xbufs, xb0_old, xb1_old, Grow, grow_old,
                       dst, dst_deps, is_out, src_deps=None):
        gd = Dep()
        alias([gd], grow_old)
        r = 2 * sub + cond
        P.op("sp", lambda e: e.dma_start(out=Grow, in_=grow[r:r + 1, :].partition_broadcast(128)),
             reads=[growd], writes=[gd], dsem=gsem)
        xbd = [Dep(), Dep()]
        alias([xbd[0]], xb0_old)
        alias([xbd[1]], xb1_old)
        sd = Dep()
        for b in range(4):
            k = b % 2
            xb = xbufs[k]
            gblk = (c_lo // 128) + b
            rd = [src_deps[gblk]] if src_deps is not None else []
            P.op("sp", lambda e, xb=xb, b=b: e.dma_start(out=xb, in_=xsrc[c_lo + b * 128:c_lo + (b + 1) * 128, :]),
                 reads=rd, writes=[xbd[k]], dsem=xsem[k])
            c0 = 8 + b * 4
            P.op("dve", lambda e, b=b, c0=c0: e.reduce_sum(stat[:, c0:c0 + 1], stat2[:, b, :], axis=AX.X),
                 reads=[s2d[b]], writes=[sd])
            P.op("act", lambda e, c0=c0: e.activation(stat[:, c0 + 1:c0 + 2], stat[:, c0:c0 + 1], AF.Sqrt, bias=EPS, scale=1.0 / D),
                 reads=[sd], writes=[sd])
            P.op("dve", lambda e, c0=c0: e.reciprocal(stat[:, c0 + 2:c0 + 3], stat[:, c0 + 1:c0 + 2]), reads=[sd], writes=[sd])
            if fbuf.dtype == F32:
                P.op("dve", lambda e, b=b: e.tensor_tensor(fbuf[:, b, :], fbuf[:, b, :], Grow, ALU.mult),
                     reads=[gd], writes=[fdeps[b]])
                P.op("dve", lambda e, b=b, xb=xb, c0=c0: e.scalar_tensor_tensor(xb, fbuf[:, b, :], stat[:, c0 + 2:c0 + 3], xb,
                                                                                 ALU.mult, ALU.add),
                     reads=[fdeps[b], sd], writes=[xbd[k]])
            else:
                tb = epi_tmp[0]
                P.op("dve", lambda e, b=b, c0=c0, tb=tb: e.scalar_tensor_tensor(tb, fbuf[:, b, :], stat[:, c0 + 2:c0 + 3], Grow,
                                                                                 ALU.mult, ALU.mult),
                     reads=[fdeps[b], sd, gd], writes=[epi_tmp[1]])
                P.op("dve", lambda e, xb=xb, tb=tb: e.tensor_tensor(xb, xb, tb, ALU.add),
                     reads=[epi_tmp[1]], writes=[xbd[k]])
            ev = P.op("sp", lambda e, xb=xb, b=b: e.dma_start(out=dst[c_lo + b * 128:c_lo + (b + 1) * 128, :], in_=xb),
                      reads=[xbd[k]], writes=[dst_deps[gblk]] if not is_out else [], dsem=osem[k])
            if is_out:
                out_events.append(ev)
        tile_state["ep_deps"] = [gd, xbd[0], xbd[1], sd]

    tile_state = {"deps": [], "ep_deps": [], "hTh": []}
    epi_tmp = [None, None]
    cksem = [P.dma_sem("ck") for _ in range(4)]

    tiles_a = [("p", ti) for ti in range(n_ptiles)] + [("s", ti) for ti in range(n_stiles)]
    actx = a_make(*tiles_a[0])
    for (fL, fN, fT) in actx["lnt"]:
        fL()
        fN()
        fT()
    aepi = []
    for idx in range(len(tiles_a)):
        preconv["active"] = idx >= 1
        aepi, actx = pass_a_tile(actx, aepi, tiles_a[idx + 1] if idx + 1 < len(tiles_a) else None)
    for grp in aepi:
        for s in grp:
            s()
    preconv["active"] = False
    for sd_ in preconv["deps"]:
        sd_.w = (pcsem[0], pcsem[1])

    XN2 = rview(60480, 4096)
    FSTf = rview(64576, 4096).bitcast(F32)
    fst = [FSTf[:, k * 1024:(k + 1) * 1024].rearrange("p (b n) -> p b n", b=4) for k in range(2)]
    E_CH0 = 22
    EF = rview(11264, 8192).bitcast(F32)
    EX = [rview(19456, 8192).bitcast(F32), rview(27648, 8192).bitcast(F32)]
    EG = rview(35840, 8192).bitcast(F32)
    xbB = RK[:, 0:4096]
    h2d = [Dep() for _ in range(32)]
    actd = [Dep() for _ in range(FC)]
    xn2d = Dep()
    fstd = [[Dep() for _ in range(4)] for _ in range(2)]
    EFd, EXd, EGd = Dep(), [Dep(), Dep()], Dep()
    hThB = Dep()
    alias(h2d + actd + [xn2d, EFd, EGd] + EXd + [d for row in fstd for d in row],
          reg_users["H"] + reg_users["S"] + reg_users["Q"] + reg_users["O"])
    alias([hThB], tile_state["hTh"])

    def b_make(kind, ti):
        cond = 0 if kind == "p" else 1
        x1 = x1p if kind == "p" else x1s
        x1d = x1p_d if kind == "p" else x1s_d
        c_lo = 512 * ti
        hr = []
        if kind == "s":
            if c_lo - 1 >= 0:
                hr.append((c_lo - 1, 0))
            if c_lo + 512 < 2048:
                hr.append((c_lo + 512, 513))
        gb0 = c_lo // 128
        blocks = [dict(n=128, rows=[(c_lo + b * 128, 128, 0)], dst=[(0, 128, 1 + b * 128)], srcdeps=[x1d[gb0 + b]])
                  for b in range(4)]
        if hr:
            rows = [(row, 1, idx) for idx, (row, col) in enumerate(hr)]
            dst = [(idx, 1, col) for idx, (row, col) in enumerate(hr)]
            sdeps = [x1d[row // 128] for (row, col) in hr]
            blocks.append(dict(n=len(hr), rows=rows, dst=dst, srcdeps=sdeps))
        xbd = new_work(1)[0]
        steps = h_steps(x1, blocks, xbB, xbd, XN2, xn2d, A2[cond], B2[cond], hT2, h2d, [])

        def hth():
            if hr:
                nh0 = len(hr)
                col00 = hr[0][1]
                hsrc = hT2[:, :, 0:514:513] if nh0 == 2 else hT2[:, :, col00:col00 + 1]
                P.op("dve", lambda e: e.tensor_copy(hTh[:, :, 0:nh0], hsrc), reads=h2d, writes=[hThB])
        steps.append(hth)
        return dict(kind=kind, ti=ti, cond=cond, x1=x1, x1d=x1d, c_lo=c_lo, hr=hr, hsteps=steps,
                    ydst=yp if kind == "p" else ys)

    def b_up(ctx, epi):
        kind, hr = ctx["kind"], ctx["hr"]
        nseg, L = (2, 256) if kind == "p" else (1, 512)
        LP = L + 2
        assert len(epi) <= E_CH0 // 2
        uw = new_work(6)
        gbuf = [RK[:, 0:516], RK[:, 516:1032]]
        tbf = [RK[:, 1032:1544], RK[:, 1544:2056]]
        for k in range(2):
            P.op("dve", lambda e, k=k: e.memset(gbuf[k], 0.0), writes=[uw[k]])
        wfc = w_ffn_convT[:, :].rearrange("p (k c) -> p k c", k=3)
        cn = 0
        for jj in range(FC // 2):
            wt, wd = wload(wv_up[:, :, jj * 256:(jj + 1) * 256], ("gt", jj))
            ks = []
            for hh in range(2):
                j = 2 * jj + hh
                k = cn % 2
                cn += 1
                ks.append(k)
                pt, ptd = nextps()
                pairs = [(wt[:, kc, hh * 128:(hh + 1) * 128], hT2[:, kc, 1:513]) for kc in range(KC)]
                pe_group(pt[:, :], pairs, reads=[wd] + h2d, writes=[ptd])
                gb3 = gbuf[k][:, 0:nseg * LP].rearrange("p (s l) -> p s l", s=nseg)
                act_copy(gb3[:, :, 1:L + 1], pt[:, :].rearrange("p (s l) -> p s l", s=nseg), [ptd], [uw[k]])
                if hr:
                    hp, hpd = nextps()
                    nh_ = len(hr)
                    col0 = hr[0][1]
                    pairs = [(wt[:, kc, hh * 128:(hh + 1) * 128], hTh[:, kc, 0:nh_]) for kc in range(KC)]
                    pe_group(hp[:, 0:nh_], pairs, reads=[wd, hThB], writes=[hpd])
                    gdst = gbuf[k][:, 0:514:513] if nh_ == 2 else gbuf[k][:, col0:col0 + 1]
                    dve_copy(gdst, hp[:, 0:nh_], [hpd], [uw[k]])
                t3 = tbf[k].rearrange("p (s l) -> p s l", s=nseg)
                P.op("dve", lambda e, gb3=gb3, t3=t3, j=j: e.tensor_scalar(t3, gb3[:, :, 1:L + 1], wfc[:, 1, j:j + 1], None, ALU.mult),
                     reads=[uw[k], cd["w_ffn_convT"]], writes=[uw[2 + k]])
                P.op("dve", lambda e, gb3=gb3, t3=t3, j=j: e.scalar_tensor_tensor(t3, gb3[:, :, 0:L], wfc[:, 0, j:j + 1], t3, ALU.mult, ALU.add),
                     reads=[uw[k]], writes=[uw[2 + k]])
                P.op("dve", lambda e, gb3=gb3, t3=t3, j=j: e.scalar_tensor_tensor(t3, gb3[:, :, 2:L + 2], wfc[:, 2, j:j + 1], t3, ALU.mult, ALU.add),
                     reads=[uw[k]], writes=[uw[2 + k]])
                P.op("act", lambda e, k=k: e.activation(tbf[k], tbf[k], AF.Silu), reads=[uw[2 + k]], writes=[uw[2 + k]])
            wt, wd = wload(wv_up[:, :, DFF + jj * 256:DFF + (jj + 1) * 256], ("val", jj))
            for hh in range(2):
                j = 2 * jj + hh
                k = ks[hh]
                pt, ptd = nextps()
                pairs = [(wt[:, kc, hh * 128:(hh + 1) * 128], hT2[:, kc, 1:513]) for kc in range(KC)]
                pe_group(pt[:, :], pairs, reads=[wd] + h2d, writes=[ptd])
                P.op("dve", lambda e, pt=pt, k=k, j=j: e.tensor_tensor(actv[:, j, :], pt[:, :], tbf[k], ALU.mult),
                     reads=[ptd, uw[2 + k]], writes=[actd[j]])
            if jj < len(epi):
                for s in epi[jj]:
                    s()

    def b_down(ctx, hsteps):
        kparts = [(0, 32), (32, 64), (64, FC)]
        assert len(hsteps) <= 16
        for nt in range(16):
            pts = []
            for b in range(4):
                pt, ptd = nextps()
                pts.append((pt[:, 0:256], ptd))
            for kp, (k0, k1) in enumerate(kparts):
                wt, wd = wload(wv_down[:, k0:k1, nt * 256:(nt + 1) * 256], ("dn", nt, kp))
                for b in range(4):
                    o, od = pts[b]

                    def fn(pe, o=o, wt=wt, b=b, k0=k0, k1=k1, kp=kp):
                        for kc in range(k0, k1):
                            ins = pe.matmul(o, actv[:, kc, b * 128:(b + 1) * 128], wt[:, kc - k0, :],
                                            start=(kp == 0 and kc == k0), stop=(kp == 2 and kc == k1 - 1))
                        return ins
                    P.op("pe", fn, reads=[wd] + actd[k0:k1], writes=[od])
            k = nt % 2
            for b in range(4):
                o, od = pts[b]
                any_copy(fst[k][:, b, :], o, [od], [fstd[k][b]])
                P.op("act", lambda e, o=o, b=b, nt=nt: e.activation(junk[:, :], o, AF.Square, accum_out=stat2[:, b, nt:nt + 1]),
                     reads=[od], writes=[jdB, s2dB[b]])
            P.op("sp", lambda e, k=k, nt=nt: e.dma_start(out=fscr_v[:, :, nt * 256:(nt + 1) * 256], in_=fst[k]),
                 reads=fstd[k], writes=[fscrd], dsem=fssem)
            if nt < len(hsteps):
                hsteps[nt]()

    def b_epi(ctx):
        alias([EGd, EFd, EXd[0], EXd[1]], actd[E_CH0:])
        groups = epi_steps(EF, EX, EG, EFd, EXd, EGd, 2 + ctx["cond"], ctx["c_lo"], ctx["x1"], ctx["x1d"],
                           ctx["ydst"], None, True)
        groups[-1].append(lambda: alias(actd[E_CH0:], [EGd, EFd, EXd[0], EXd[1]]))
        return groups

    tiles_b = [("p", ti) for ti in range(n_ptiles)] + [("s", ti) for ti in range(n_stiles)]
    if do_b and tiles_b:
        ctx = b_make(*tiles_b[0])
        for s in ctx["hsteps"]:
            s()
        epi = []
        for idx in range(len(tiles_b)):
            b_up(ctx, epi)
            nctx = b_make(*tiles_b[idx + 1]) if idx + 1 < len(tiles_b) else None
            b_down(ctx, nctx["hsteps"] if nctx else [])
            epi = b_epi(ctx)
            ctx = nctx
        for grp in epi:
            for s in grp:
                s()

    fin = {}
    for ev in out_events:
        k = id(ev[0])
        if k not in fin or fin[k][1] < ev[1]:
            fin[k] = ev
    for d in x1p_d + x1s_d:
        if d.w is not None:
            k = id(d.w[0])
            if k not in fin or fin[k][1] < d.w[1]:
                fin[k] = d.w
    final_waits = list(fin.values())

    with nc.Block() as block:
        @block.tensor
        def _(e):
            P.emit("pe", e)

        @block.scalar
        def _(e):
            P.emit("act", e)

        @block.vector
        def _(e):
            P.emit("dve", e)

        @block.gpsimd
        def _(e):
            P.emit("pool", e)

        @block.sync
        def _(e):
            P.emit("sp", e)
            for (s_, v_) in final_waits:
                e.wait_ge(s_, v_)
    stack.close()
    return nc


def _consts():
    ident = np.eye(128, dtype=np.float32)
    perm = np.zeros((128, 128), np.float32)
    for k in range(128):
        perm[k, k ^ 32] = 1.0
    i = np.arange(128)[:, None]
    j = np.arange(128)[None, :]
    mL = np.where(j >= i, 0.0, NEGBIG).astype(np.float32)
    mR = np.where(j <= i, 0.0, NEGBIG).astype(np.float32)
    mask = np.concatenate([mL, np.zeros((128, 128), np.float32), mR], axis=1)
    L = 2048
    t = np.arange(L)
    row = (t // 64).astype(np.float32)
    col = (t % 64).astype(np.float32)
    nf = 32
    inv = (10000.0 ** (-np.arange(nf, dtype=np.float32) / nf)).astype(np.float32)
    ang = np.concatenate([row[:, None] * inv, col[:, None] * inv], axis=-1).astype(np.float32)
    cos, sin = np.cos(ang), np.sin(ang)
    C = np.zeros((128, L), np.float32)
    S = np.zeros((128, L), np.float32)
    for d in range(128):
        axis, half, f = d // 64, (d % 64) // 32, d % 32
        C[d] = cos[:, axis * 32 + f]
        S[d] = (-sin[:, axis * 32 + f]) if half == 0 else sin[:, axis * 32 + f]
    return ident, perm, mask, C, S


def _fm(v, nchunk):
    return np.ascontiguousarray(v.reshape(nchunk, 128).T)


_NC_CACHE = {}


def make_in_maps(x_prompt, x_sample, cache_k, cache_v, c, c_ctx, w_mod, b_mod, g_pre_mix, w_in, w_sconv, attn_sink,
                 w_attn_o, w_conv_o, w_mix_out, g_post_mix, g_pre_ffn, w_ffn_up, w_ffn_conv, w_ffn_down, g_post_ffn,
                 cores=range(8)):
    f = lambda a: np.ascontiguousarray(np.asarray(a, dtype=np.float32))
    ident, perm, mask, C, S = _consts()
    gvec = np.concatenate([_fm(f(g)[0], 32) for g in (g_pre_mix, g_post_mix, g_pre_ffn, g_post_ffn)], axis=1)
    shared = {
        "w_mod": f(w_mod)[0], "b_modT": _fm(f(b_mod)[0], 192), "gvec": np.ascontiguousarray(gvec),
        "w_in": f(w_in)[0],
        "w_sconvT": np.ascontiguousarray(np.concatenate([_fm(f(w_sconv)[0, k], 16) for k in range(3)], axis=1)),
        "sinkb": np.ascontiguousarray(np.broadcast_to(f(attn_sink)[0][None, :], (128, 32))),
        "w_attn_o": f(w_attn_o)[0], "w_conv_o": f(w_conv_o)[0], "w_mix_out": f(w_mix_out)[0],
        "w_ffn_up": f(w_ffn_up)[0],
        "w_ffn_convT": np.ascontiguousarray(np.concatenate([_fm(f(w_ffn_conv)[0, k], FC) for k in range(3)], axis=1)),
        "w_ffn_down": f(w_ffn_down)[0],
        "ident": ident, "perm": perm, "maskc": mask, "ropeC": C, "ropeS": S,
    }
    xpf, xsf, ckf, cvf, cf, cctx = f(x_prompt), f(x_sample), f(cache_k), f(cache_v), f(c), f(c_ctx)
    in_maps = []
    for i in cores:
        cond = np.stack([cctx, cf[i]], axis=0)
        condT = np.ascontiguousarray(cond.reshape(2, 32, 128).transpose(2, 1, 0).reshape(128, 64))
        m = dict(shared)
        m["xp"] = np.ascontiguousarray(xpf[4 * i:4 * i + 4].reshape(1024, D))
        m["xs"] = np.ascontiguousarray(xsf[i])
        m["ck"] = np.ascontiguousarray(ckf[i, 0].reshape(PAST, 1024))
        m["cv"] = np.ascontiguousarray(cvf[i, 0].reshape(PAST, 1024))
        m["condT"] = condT
        in_maps.append(m)
    return in_maps


def kernel(**inputs):
    in_maps = make_in_maps(**inputs)
    if "nc" not in _NC_CACHE:
        _NC_CACHE["nc"] = build_program()
    nc = _NC_CACHE["nc"]
    res = run_bass_kernel_spmd(nc, in_maps, core_ids=list(range(8)))
    r = res.results
    y_prompt = np.concatenate([r[i]["yp"].reshape(4, 256, D) for i in range(8)], axis=0)
    y_sample = np.stack([r[i]["ys"] for i in range(8)], axis=0)
    new_k = np.concatenate([r[i]["nk"].reshape(4, 1, 256, 8, 128) for i in range(8)], axis=0)
    new_v = np.concatenate([r[i]["nv"].reshape(4, 1, 256, 8, 128) for i in range(8)], axis=0)
    return (y_prompt.astype(np.float32), y_sample.astype(np.float32), new_k.astype(np.float32), new_v.astype(np.float32))
```
